# Optimizing a Trainium2 kernel written in Bass

```python
import jax, jax.numpy as jnp
from jax import lax
import numpy as np

D_MODEL = 1024
BATCH = 8
SEQ = 2048
DEPTH = 1
DEC_BATCH = 128
DEC_SEQ = 4
PAST_LEN = 16384
PAGE_SIZE = 128

CHUNK = 128
D_A = D_MODEL
GROUPS_A = 8
GW_A = D_A // GROUPS_A
HEAD_B = 64
D_B = D_MODEL
H_B = D_B // HEAD_B
DECAY_LORA = 64
ICLR_LORA = 64
GATE_LORA = 128
H_C = 4
D_C = D_MODEL
DH_C = D_C // H_C
N_MEM = 256
D_FF = 4 * D_MODEL
N_BRANCH = 3
P_SHIFT = 3 * D_B + DECAY_LORA + ICLR_LORA + GATE_LORA
P_IN = 2 * D_A + P_SHIFT + D_C + N_BRANCH * D_MODEL
RMS_EPS = 1e-6
LN_EPS = 1e-5
GN_EPS = 64e-5

kernel_name = "gmlp_rwkv7_memxattn_hybrid_step"


def _split(z, widths):
    outs, o = [], 0
    for w in widths:
        outs.append(z[..., o:o + w])
        o += w
    return outs


def rmsnorm(x, g):
    xf = x.astype(jnp.float32)
    y = xf * lax.rsqrt(jnp.mean(xf * xf, axis=-1, keepdims=True) + RMS_EPS)
    return (y * g.astype(jnp.float32)).astype(x.dtype)


def layernorm(x, g, b):
    xf = x.astype(jnp.float32)
    mu = jnp.mean(xf, axis=-1, keepdims=True)
    var = jnp.mean(jnp.square(xf - mu), axis=-1, keepdims=True)
    y = (xf - mu) * lax.rsqrt(var + LN_EPS)
    return (y * g.astype(jnp.float32) + b.astype(jnp.float32)).astype(x.dtype)


def spatial_gate(vn, sg_w, sg_b):
    B, T, _ = vn.shape
    nc = -(-T // CHUNK)
    vp = jnp.pad(vn, ((0, 0), (0, nc * CHUNK - T), (0, 0))).reshape(B, nc, CHUNK, GROUPS_A, GW_A)
    mask = jnp.tril(jnp.ones((CHUNK, CHUNK), dtype=bool))
    w = jnp.where(mask[None], sg_w, jnp.zeros((), sg_w.dtype))
    s = jnp.einsum('gts,bcsgh->bctgh', w, vp) + sg_b.T[None, None, :, :, None]
    return s.reshape(B, nc * CHUNK, D_A)[:, :T]


def wkv7_scan(r, decay, k, v, a_vec, b_vec, s0):
    def step(S, inp):
        r_t, w_t, k_t, v_t, a_t, b_t = inp
        sa = jnp.einsum('bhij,bhj->bhi', S, a_t)
        S = S * w_t[:, :, None, :] + sa[..., None] * b_t[:, :, None, :] + v_t[..., None] * k_t[:, :, None, :]
        return S, jnp.einsum('bhij,bhj->bhi', S, r_t)
    xs = tuple(jnp.moveaxis(t, 1, 0) for t in (r, decay, k, v, a_vec, b_vec))
    S, ys = lax.scan(step, s0, xs)
    return jnp.moveaxis(ys, 0, 1), S


def rwkv7_branch(rz, kz, vz, wl, al, gl, wkv0, p):
    B, T, _ = rz.shape
    f32 = jnp.float32
    w_log = -jax.nn.softplus(-(p['w0'] + jnp.tanh(wl) @ p['w_w2']).astype(f32)) - 0.5
    decay = jnp.exp(-jnp.exp(w_log))
    a = jax.nn.sigmoid((p['a0'] + al @ p['w_a2']).astype(f32))
    g = jax.nn.sigmoid(gl) @ p['w_g2']

    def heads(t):
        return t.astype(f32).reshape(B, T, H_B, HEAD_B)

    kk = heads(kz * p['k_k'])
    kk = kk * lax.rsqrt(jnp.sum(kk * kk, axis=-1, keepdims=True) + 1e-12)
    k = kz.astype(f32) * (1.0 + (a - 1.0) * p['k_a'].astype(f32))
    r_h, k_h, v_h, a_h = heads(rz), heads(k), heads(vz), heads(a)
    y, S = wkv7_scan(r_h, heads(decay), k_h, v_h, -kk, kk * a_h, wkv0.astype(f32))
    mu = jnp.mean(y, axis=-1, keepdims=True)
    var = jnp.mean(jnp.square(y - mu), axis=-1, keepdims=True)
    y = (y - mu) * lax.rsqrt(var + GN_EPS)
    y = y * p['lnx_g'].astype(f32).reshape(H_B, HEAD_B) + p['lnx_b'].astype(f32).reshape(H_B, HEAD_B)
    y = y + jnp.sum(r_h * k_h * p['r_k'].astype(f32), axis=-1, keepdims=True) * v_h
    y = y.reshape(B, T, D_B).astype(rz.dtype) * g
    return y, S


def memory_kv(mem, g, wk, wv):
    B, M, _ = mem.shape
    hm = rmsnorm(mem, g)
    return (hm @ wk).reshape(B, M, H_C, DH_C), (hm @ wv).reshape(B, M, H_C, DH_C)


def memory_cross_attn(qz, mk, mv):
    B, T, _ = qz.shape
    q = qz.reshape(B, T, H_C, DH_C)
    s = jnp.einsum('bthd,bmhd->bhtm', q, mk).astype(jnp.float32) * (DH_C ** -0.5)
    pr = jax.nn.softmax(s, axis=-1).astype(mv.dtype)
    return jnp.einsum('bhtm,bmhd->bthd', pr, mv).reshape(B, T, D_C)


def hybrid_layer(x, mk, mv, shift_prev, wkv_prev, p):
    B, T, _ = x.shape
    h = rmsnorm(x, p['norm1_g'])
    z = h @ p['w_in']
    uz, vz, sz, qz, gz = _split(z, (D_A, D_A, P_SHIFT, D_C, N_BRANCH * D_MODEL))
    prev = jnp.concatenate([shift_prev[:, None].astype(sz.dtype), sz[:, :-1]], axis=1)
    ss = sz + (prev - sz) * p['shift_mu']
    rz, kz, vbz, wl, al, gl = _split(ss, (D_B, D_B, D_B, DECAY_LORA, ICLR_LORA, GATE_LORA))
    u = jax.nn.gelu(uz)
    vn = layernorm(jax.nn.gelu(vz), p['ln_v_g'], p['ln_v_b'])
    y_a = u * spatial_gate(vn, p['sg_w'], p['sg_b'])
    y_b, wkv_new = rwkv7_branch(rz, kz, vbz, wl, al, gl, wkv_prev, p)
    y_c = memory_cross_attn(qz, mk, mv)
    branches = jnp.stack([y_a, y_b, y_c], axis=0)
    proj = jnp.einsum('nbtc,ncd->nbtd', branches, p['w_branch'])
    gates = jax.nn.sigmoid(gz.reshape(B, T, N_BRANCH, D_MODEL))
    merged = jnp.einsum('nbtd,btnd->btd', proj, gates)
    x = x + merged @ p['w_out']
    h2 = rmsnorm(x, p['norm2_g'])
    x = x + jnp.square(jax.nn.relu(h2 @ p['w_up'])) @ p['w_down']
    return x, sz[:, -1], wkv_new.astype(x.dtype), vn


def setup_inputs(seed: int = 0) -> dict:
    key = jax.random.key(seed)
    ks = iter(list(jax.random.split(key, 40)))
    f32 = jnp.float32
    L = DEPTH

    def nrm(shape, scale):
        return jax.random.normal(next(ks), shape, f32) * scale

    def gain(shape):
        return 1.0 + nrm(shape, 0.02)

    def unif(shape, lo, hi):
        return jax.random.uniform(next(ks), shape, f32, lo, hi)

    return {
        "x_prompt": nrm((BATCH, SEQ, D_MODEL), 1.0),
        "x_sample": nrm((DEC_BATCH, DEC_SEQ, D_MODEL), 1.0),
        "mem_prompt": nrm((BATCH, N_MEM, D_MODEL), 1.0),
        "cache_mem_k": nrm((L, DEC_BATCH, N_MEM, H_C, DH_C), 1.0),
        "cache_mem_v": nrm((L, DEC_BATCH, N_MEM, H_C, DH_C), 1.0),
        "state_shift": nrm((L, DEC_BATCH, P_SHIFT), 1.0),
        "state_wkv": nrm((L, DEC_BATCH, H_B, HEAD_B, HEAD_B), 0.1),
        "norm1_g": gain((L, D_MODEL)),
        "w_in": nrm((L, D_MODEL, P_IN), D_MODEL ** -0.5),
        "shift_mu": unif((L, P_SHIFT), 0.0, 1.0),
        "ln_v_g": gain((L, D_A)),
        "ln_v_b": nrm((L, D_A), 0.02),
        "sg_w": nrm((L, GROUPS_A, CHUNK, CHUNK), CHUNK ** -0.5),
        "sg_b": gain((L, GROUPS_A, CHUNK)),
        "w0": unif((L, D_B), -6.0, -1.0),
        "w_w2": nrm((L, DECAY_LORA, D_B), 0.5 * DECAY_LORA ** -0.5),
        "a0": nrm((L, D_B), 0.1),
        "w_a2": nrm((L, ICLR_LORA, D_B), ICLR_LORA ** -0.5),
        "w_g2": nrm((L, GATE_LORA, D_B), GATE_LORA ** -0.5),
        "k_k": 0.85 + nrm((L, D_B), 0.02),
        "k_a": gain((L, D_B)),
        "r_k": nrm((L, H_B, HEAD_B), 0.1),
        "lnx_g": gain((L, D_B)),
        "lnx_b": nrm((L, D_B), 0.02),
        "mem_norm_g": gain((L, D_MODEL)),
        "w_mem_k": nrm((L, D_MODEL, D_C), D_MODEL ** -0.5),
        "w_mem_v": nrm((L, D_MODEL, D_C), D_MODEL ** -0.5),
        "w_branch": nrm((L, N_BRANCH, D_MODEL, D_MODEL), D_MODEL ** -0.5),
        "w_out": nrm((L, D_MODEL, D_MODEL), D_MODEL ** -0.5),
        "norm2_g": gain((L, D_MODEL)),
        "w_up": nrm((L, D_MODEL, D_FF), D_MODEL ** -0.5),
        "w_down": nrm((L, D_FF, D_MODEL), D_FF ** -0.5),
        "norm_f_g": gain((D_MODEL,)),
    }


def reference(x_prompt, x_sample, mem_prompt, cache_mem_k, cache_mem_v, state_shift, state_wkv,
              norm1_g, w_in, shift_mu, ln_v_g, ln_v_b, sg_w, sg_b, w0, w_w2, a0, w_a2, w_g2,
              k_k, k_a, r_k, lnx_g, lnx_b, mem_norm_g, w_mem_k, w_mem_v, w_branch, w_out,
              norm2_g, w_up, w_down, norm_f_g):
    xp, xs = x_prompt, x_sample
    Bp = x_prompt.shape[0]
    p_shift, p_wkv, p_mk, p_mv = [], [], [], []
    s_shift, s_wkv, s_gv = [], [], []
    for l in range(DEPTH):
        p = dict(norm1_g=norm1_g[l], w_in=w_in[l], shift_mu=shift_mu[l], ln_v_g=ln_v_g[l],
                 ln_v_b=ln_v_b[l], sg_w=sg_w[l], sg_b=sg_b[l], w0=w0[l], w_w2=w_w2[l], a0=a0[l],
                 w_a2=w_a2[l], w_g2=w_g2[l], k_k=k_k[l], k_a=k_a[l], r_k=r_k[l], lnx_g=lnx_g[l],
                 lnx_b=lnx_b[l], w_branch=w_branch[l], w_out=w_out[l], norm2_g=norm2_g[l],
                 w_up=w_up[l], w_down=w_down[l])
        mk_p, mv_p = memory_kv(mem_prompt, mem_norm_g[l], w_mem_k[l], w_mem_v[l])
        shift0 = jnp.zeros((Bp, P_SHIFT), x_prompt.dtype)
        wkv0 = jnp.zeros((Bp, H_B, HEAD_B, HEAD_B), jnp.float32)
        xp, sh_p, wkv_p, _ = hybrid_layer(xp, mk_p, mv_p, shift0, wkv0, p)
        xs, sh_s, wkv_s, vn_s = hybrid_layer(xs, cache_mem_k[l], cache_mem_v[l],
                                             state_shift[l], state_wkv[l], p)
        p_shift.append(sh_p); p_wkv.append(wkv_p); p_mk.append(mk_p); p_mv.append(mv_p)
        s_shift.append(sh_s); s_wkv.append(wkv_s); s_gv.append(vn_s)
    y_prompt = rmsnorm(xp, norm_f_g)
    y_sample = rmsnorm(xs, norm_f_g)
    return (y_prompt, y_sample, jnp.stack(p_shift), jnp.stack(p_wkv), jnp.stack(p_mk),
            jnp.stack(p_mv), jnp.stack(s_shift), jnp.stack(s_wkv), jnp.stack(s_gv))
```

```python
import contextlib
import numpy as np
import concourse.bass as bass
import concourse.mybir as mybir
from concourse.bass_utils import run_bass_kernel_spmd

F32 = mybir.dt.float32
F32R = mybir.dt.float32r
BF16 = mybir.dt.bfloat16
AF = mybir.ActivationFunctionType
ALU = mybir.AluOpType
AX = mybir.AxisListType

NCORES = 8
D = 1024
KC = 8
SEQ = 2048
NSQ = 16
LS = 4
NMEM = 256
PSH = 3328
NSC = 26
PIN = 9472
DFF = 4096
NTP = 256
NGP = SEQ // NTP
RMS_EPS = 1e-6
LN_EPS = 1e-5
GN_EPS = 64e-5
EXPM05 = float(np.exp(-0.5))

ENGS = ("pe", "act", "dve", "pool", "sp")
N_DMA_SLOTS = 36
GR = 32

DEBUG = {}


class Reg:
    __slots__ = ("sp", "ranges")

    def __init__(self, sp, ranges):
        self.sp = sp
        self.ranges = ranges

    def keys(self):
        base = {"F": 0, "R": 100000, "P": 200000, "B": 300000, "D": 400000}[self.sp]
        out = []
        for lo, n in self.ranges:
            out.extend(range(base + lo // GR, base + (lo + n - 1) // GR + 1))
        return out


class Sched:
    def __init__(self):
        self.streams = {e: [] for e in ENGS}
        self.count = {e: 0 for e in ENGS}
        self.clock = {e: {} for e in ENGS}
        self.tok_clock = {}
        self.last_write = {}
        self.readers = {}
        self.n_dma = 0
        self.dma_slot_val = [0] * N_DMA_SLOTS
        self.n_waits = 0
        self.sw_uses = {}
        self.n_once = 0
        self.sw_tick = {}

    def _deps(self, rkeys, wkeys):
        need = {}
        lw = self.last_write
        rd = self.readers
        for k in rkeys:
            t = lw.get(k)
            if t is not None and need.get(t[0], 0) < t[1]:
                need[t[0]] = t[1]
        for k in wkeys:
            t = lw.get(k)
            if t is not None and need.get(t[0], 0) < t[1]:
                need[t[0]] = t[1]
            r = rd.get(k)
            if r:
                for s, v in r.items():
                    if need.get(s, 0) < v:
                        need[s] = v
        for s in list(need):
            tk = self.sw_tick.get(s)
            if tk is not None and need.get(tk[0], 0) < tk[1]:
                need[tk[0]] = tk[1]
        return need

    def _emit_waits(self, eng, need):
        clk = self.clock[eng]
        for s, v in need.items():
            if eng == "pe" and s == "pe":
                continue
            if clk.get(s, 0) >= v:
                continue
            self.streams[eng].append(("wait", s, v))
            self.n_waits += 1
            clk[s] = v
            snap = self.tok_clock.get((s, v))
            if snap:
                for s2, v2 in snap.items():
                    if clk.get(s2, 0) < v2:
                        clk[s2] = v2

    def _record(self, tok, rkeys, wkeys):
        lw = self.last_write
        rd = self.readers
        ws = set(wkeys)
        for k in wkeys:
            lw[k] = tok
            rd[k] = None
        s, v = tok
        for k in rkeys:
            if k in ws:
                continue
            r = rd.get(k)
            if r is None:
                rd[k] = {s: v}
            elif r.get(s, 0) < v:
                r[s] = v

    @staticmethod
    def _keys(regs):
        out = []
        for r in regs:
            out.extend(r.keys())
        return out

    def op(self, eng, fn, reads=(), writes=()):
        rk = self._keys(reads)
        wk = self._keys(writes)
        self._emit_waits(eng, self._deps(rk, wk))
        self.count[eng] += 1
        tok = (eng, self.count[eng])
        self.streams[eng].append(("op", fn, eng))
        self.tok_clock[tok] = dict(self.clock[eng])
        self._record(tok, rk, wk)

    def dma_sw(self, q, fn, ring, tick, reads=(), writes=()):
        rk = self._keys(reads)
        wk = self._keys(writes)
        need = self._deps(rk, wk)
        self._emit_waits(q, need)
        self.sw_uses[ring] = self.sw_uses.get(ring, 0) + 1
        skey = ("w", ring, self.sw_uses[ring])
        if self.sw_uses[ring] > 1:
            self.streams[q].append(("clear", ("w", ring)))
            self.count[q] += 1
            self.streams[q].append(("op", tick, q))
            self.tok_clock[(q, self.count[q])] = dict(self.clock[q])
            self.sw_tick[skey] = (q, self.count[q])
        tok = (skey, 16)
        self.streams[q].append(("dma", fn, skey))
        self.tok_clock[tok] = dict(self.clock[q])
        self._record(tok, rk, wk)

    def dma_once(self, q, fn, reads=(), writes=()):
        rk = self._keys(reads)
        wk = self._keys(writes)
        self._emit_waits(q, self._deps(rk, wk))
        self.n_once += 1
        skey = ("o", self.n_once)
        tok = (skey, 16)
        self.streams[q].append(("dma", fn, skey))
        self.tok_clock[tok] = dict(self.clock[q])
        self._record(tok, rk, wk)

    def dma(self, q, fn, reads=(), writes=()):
        rk = self._keys(reads)
        wk = self._keys(writes)
        need = self._deps(rk, wk)
        slot = self.n_dma % N_DMA_SLOTS
        self.n_dma += 1
        prev = self.dma_slot_val[slot]
        skey = ("d", slot)
        if prev > 0 and need.get(skey, 0) < prev:
            need[skey] = prev
        self._emit_waits(q, need)
        val = prev + 16
        self.dma_slot_val[slot] = val
        tok = (skey, val)
        self.streams[q].append(("dma", fn, skey))
        self.tok_clock[tok] = dict(self.clock[q])
        self._record(tok, rk, wk)

    def finish(self, q="sp"):
        need = {}
        for slot in range(N_DMA_SLOTS):
            if self.dma_slot_val[slot] > 0:
                need[("d", slot)] = self.dma_slot_val[slot]
        for e in ("pe", "act", "dve", "pool"):
            if self.count[e] > 0:
                need[e] = self.count[e]
        for i in range(1, self.n_once + 1):
            need[("o", i)] = 16
        self._emit_waits(q, need)

    def emit(self, nc):
        with contextlib.ExitStack() as st:
            sems = {}
            for e in ("pe", "act", "dve", "pool"):
                sems[e] = st.enter_context(nc.semaphore("s_" + e))
            for i in range(N_DMA_SLOTS):
                sems[("d", i)] = st.enter_context(nc.semaphore("s_d%d" % i))
            for i in range(1, self.n_once + 1):
                sems[("o", i)] = st.enter_context(nc.semaphore("s_o%d" % i))
            block = st.enter_context(nc.Block())

            def hw(key):
                return sems[key[:2]] if key[0] == "w" else sems[key]

            def run(engobj, items):
                for it in items:
                    if it[0] == "wait":
                        engobj.wait_ge(hw(it[1]), it[2])
                    elif it[0] == "op":
                        it[1](engobj).then_inc(sems[it[2]], 1)
                    elif it[0] == "clear":
                        engobj.sem_clear(sems[it[1]])
                    else:
                        it[1](engobj).then_inc(hw(it[2]), 16)

            @block.tensor
            def _(e):
                run(e, self.streams["pe"])

            @block.scalar
            def _(e):
                run(e, self.streams["act"])

            @block.vector
            def _(e):
                run(e, self.streams["dve"])

            @block.gpsimd
            def _(e):
                run(e, self.streams["pool"])

            @block.sync
            def _(e):
                run(e, self.streams["sp"])


class Arena:
    def __init__(self, sp, tensor_ap, size):
        self.sp = sp
        self.A = tensor_ap
        self.tensor = tensor_ap.tensor
        self.size = size
        self.top = 0
        self.peak = 0

    def alloc(self, n):
        lo = self.top
        self.top += n
        self.peak = max(self.peak, self.top)
        if self.top > self.size:
            raise RuntimeError("arena %s overflow: %d > %d" % (self.sp, self.top, self.size))
        return Buf(self, lo, n)

    def mark(self):
        return self.top

    def release(self, m):
        self.top = m


class Buf:
    def __init__(self, arena, lo, n):
        self.ar = arena
        self.lo = lo
        self.n = n
        self.reg = Reg(arena.sp, [(lo, n)])

    def sub(self, off, n):
        assert off + n <= self.n, (off, n, self.n)
        return Buf(self.ar, self.lo + off, n)

    def ap(self, p0=0, p1=128, c0=0, c1=None):
        c1 = self.n if c1 is None else c1
        return self.ar.A[p0:p1, self.lo + c0:self.lo + c1]

    def v3(self, b, p0=0, p1=128):
        return self.ap(p0, p1).rearrange("p (a b) -> p a b", b=b)

    def v4(self, b, c, p0=0, p1=128):
        return self.ap(p0, p1).rearrange("p (a b c) -> p a b c", b=b, c=c)

    def pat(self, dims, off=0, p0=0, p1=128):
        base = self.ar.A[p0:p1, self.lo + off:self.lo + off + 1]
        return bass.AP(self.ar.tensor, base.offset, [[self.ar.size, p1 - p0]] + [list(d) for d in dims])

    def rsub(self, ranges):
        return Reg(self.ar.sp, [(self.lo + o, n) for o, n in ranges])


class F32View:
    def __init__(self, buf):
        self.buf = buf
        self.reg = buf.reg
        self.full = buf.ap().bitcast(F32)

    def ap(self, p0=0, p1=128, c0=0, c1=None):
        c1 = self.full.shape[1] if c1 is None else c1
        return self.full[p0:p1, c0:c1]

    def v3(self, b, p0=0, p1=128):
        return self.full[p0:p1, :].rearrange("p (a b) -> p a b", b=b)

    def sub(self, off, n):
        return self


def cols_reg(buf, rowlen, c0, C, nrows):
    return buf.rsub([(r * rowlen + c0, C) for r in range(nrows)])


class Prog:
    def __init__(self, dbg=None):
        self.dbg = dbg or {}
        self.nc = bass.Bass("TRN2", target_bir_lowering=False)
        self.S = Sched()
        self.ins = {}
        self.outs = {}
        self.bank_rr = 0
        self.alt = 0
        self.held = set()

    def din(self, name, shape):
        t = self.nc.dram_tensor(name, list(shape), F32, kind="ExternalInput").ap()
        self.ins[name] = t
        return t

    def dout(self, name, shape):
        t = self.nc.dram_tensor(name, list(shape), F32, kind="ExternalOutput").ap()
        self.outs[name] = t
        return t

    def bank(self):
        b = self.bank_rr
        while b in self.held:
            b = (b + 1) % 8
        self.bank_rr = (b + 1) % 8
        return Buf(self.PS, b * 512, 512)

    def hold(self, buf):
        for b in range(buf.lo // 512, (buf.lo + buf.n - 1) // 512 + 1):
            self.held.add(b)

    def unhold(self, buf):
        for b in range(buf.lo // 512, (buf.lo + buf.n - 1) // 512 + 1):
            self.held.discard(b)

    def bank2(self):
        b = self.bank_rr
        if b % 2:
            b = (b + 1) % 8
        while b in self.held or (b + 1) in self.held:
            b = (b + 2) % 8
        self.bank_rr = (b + 2) % 8
        return Buf(self.PS, b * 512, 1024)

    def banks_for(self, n):
        return self.bank() if n <= 512 else self.bank2()

    def ev(self):
        self.alt ^= 1
        return "act" if self.alt else "dve"

    def op(self, eng, fn, reads=(), writes=()):
        self.S.op(eng, fn, [getattr(r, 'reg', r) for r in reads],
                  [getattr(w, 'reg', w) for w in writes])

    def dma(self, q, out, in_, reads=(), writes=(), ring=None, **kw):
        rr = [getattr(r, 'reg', r) for r in reads]
        ww = [getattr(w, 'reg', w) for w in writes]
        if q == "pool":
            self.S.dma_once(q, lambda e: e.dma_start(out=out, in_=in_, **kw), rr, ww)
        else:
            self.S.dma(q, lambda e: e.dma_start(out=out, in_=in_, **kw), rr, ww)

    def mm(self, out, lhsT, rhs, start, stop, reads, writes):
        self.op("pe", lambda e: e.matmul(out, lhsT=lhsT, rhs=rhs, start=start, stop=stop), reads, writes)

    def tr(self, out, in_, ident, reads, writes):
        self.op("pe", lambda e: e.transpose(out=out, in_=in_, identity=ident), reads, writes)

    def act(self, out, in_, func, reads, writes, **kw):
        self.op("act", lambda e: e.activation(out=out, in_=in_, func=func, **kw), reads, writes)

    def copy(self, eng, out, in_, reads, writes):
        if eng == "act":
            self.op("act", lambda e: e.activation(out=out, in_=in_, func=AF.Copy), reads, writes)
        else:
            self.op(eng, lambda e: e.tensor_copy(out=out, in_=in_), reads, writes)

    def tt(self, eng, out, in0, in1, op, reads, writes):
        self.op(eng, lambda e: e.tensor_tensor(out=out, in0=in0, in1=in1, op=op), reads, writes)

    def ts(self, eng, out, in0, s1, s2, op0, op1, reads, writes):
        if s2 is None:
            self.op(eng, lambda e: e.tensor_scalar(out=out, in0=in0, scalar1=s1, scalar2=None, op0=op0), reads, writes)
        else:
            self.op(eng, lambda e: e.tensor_scalar(out=out, in0=in0, scalar1=s1, scalar2=s2, op0=op0, op1=op1),
                    reads, writes)

    def stt(self, eng, out, in0, scalar, in1, op0, op1, reads, writes):
        eng = "dve"
        self.op(eng, lambda e: e.scalar_tensor_tensor(out=out, in0=in0, scalar=scalar, in1=in1, op0=op0, op1=op1),
                reads, writes)

    def dump(self, name, ap, buf, shape):
        if not self.dbg.get(name):
            return
        t = self.dout("dbg_" + name, shape)
        self.dma("sp", t, ap, reads=[buf])


def make_consts():
    c = {}
    idx = np.arange(128)
    c["ident"] = np.eye(128, dtype=np.float32)
    blk = (idx[:, None] // 64 == idx[None, :] // 64).astype(np.float32)
    c["blk64"] = blk
    ms = (idx[:, None] < idx[None, :]).astype(np.float32)
    mi = (idx[:, None] <= idx[None, :]).astype(np.float32)
    mst = (idx[:, None] > idx[None, :]).astype(np.float32)
    c["mS4"] = np.tile(ms, (1, 4))
    c["mI4"] = np.tile(mi, (1, 4))
    c["mST4"] = np.tile(mst, (1, 4))
    seg = np.ones((128, 8, 128), np.float32)
    seg[:, :, 0] = 0.0
    c["segP"] = seg.reshape(128, 1024)
    i64 = np.arange(64)
    same = (i64[:, None] // LS == i64[None, :] // LS)
    z = np.zeros((128, 256), np.float32)
    a = z.copy(); a[:64] = np.tile((same & (i64[:, None] < i64[None, :])).astype(np.float32), (1, 4)); c["mS4s"] = a
    a = z.copy(); a[:64] = np.tile((same & (i64[:, None] <= i64[None, :])).astype(np.float32), (1, 4)); c["mI4s"] = a
    a = z.copy(); a[:64] = np.tile((same & (i64[:, None] > i64[None, :])).astype(np.float32), (1, 4)); c["mST4s"] = a
    seg = np.ones((128, 8, NSQ, LS), np.float32)
    seg[:, :, :, 0] = 0.0
    c["segS"] = seg.reshape(128, 512)
    e = np.zeros((128, 64), np.float32)
    for t in range(64):
        e[t % LS, t] = 1.0
    c["esel"] = e
    rm = np.zeros((128, NSQ), np.float32)
    for t in range(64):
        rm[t, t // LS] = 1.0
    c["rowmask"] = rm
    names = ["ident", "blk64", "mS4", "mI4", "mST4", "segP", "mS4s", "mI4s", "mST4s", "segS", "esel", "rowmask"]
    offs = {}
    o = 0
    for n in names:
        offs[n] = (o, c[n].shape[1])
        o += c[n].shape[1]
    arr = np.concatenate([c[n] for n in names], axis=1).astype(np.float32)
    return arr, offs


CONSTS, COFF = make_consts()
NCONST = CONSTS.shape[1]

VEC_NAMES = ["norm1_g", "norm2_g", "norm_f_g", "mem_norm_g", "w0", "a0", "k_k", "k_a", "r_k", "lnx_g", "lnx_b"]


def build_program(dbg=None, ngroups=NGP, do_sample=True):
    P = Prog(dbg)
    nc = P.nc
    xp = P.din("xp", [SEQ, D])
    xs = P.din("xs", [NSQ * LS, D])
    memp = P.din("memp", [NMEM, D])
    ck = P.din("ck", [NSQ, NMEM, D])
    cv = P.din("cv", [NSQ, NMEM, D])
    sshift = P.din("sshift", [NSQ, PSH])
    swkv = P.din("swkv", [NSQ, 16, 64, 64])
    consts = P.din("consts", [128, NCONST])
    onesd = P.din("onesd", [128, 128])
    vec = {n: P.din(n, [D]) for n in VEC_NAMES}
    shift_mu = P.din("shift_mu", [PSH])
    ln_v_g = P.din("ln_v_g", [D])
    ln_v_b = P.din("ln_v_b", [D])
    sg_w = P.din("sg_w", [8, 128, 128])
    sg_b = P.din("sg_b", [8, 128])
    w_in = P.din("w_in", [D, PIN])
    w_w2 = P.din("w_w2", [64, D])
    w_a2 = P.din("w_a2", [64, D])
    w_g2 = P.din("w_g2", [128, D])
    w_mem_k = P.din("w_mem_k", [D, D])
    w_mem_v = P.din("w_mem_v", [D, D])
    w_branch = P.din("w_branch", [3, D, D])
    w_out = P.din("w_out", [D, D])
    w_up = P.din("w_up", [D, DFF])
    w_down = P.din("w_down", [DFF, D])

    o_yp = P.dout("o_yp", [SEQ, D])
    o_ys = P.dout("o_ys", [NSQ * LS, D])
    o_pshift = P.dout("o_pshift", [PSH])
    o_pwkv = P.dout("o_pwkv", [16, 64, 64])
    o_pmk = P.dout("o_pmk", [NMEM, D])
    o_pmv = P.dout("o_pmv", [NMEM, D])
    o_sshift = P.dout("o_sshift", [NSQ, PSH])
    o_swkv = P.dout("o_swkv", [NSQ, 16, 64, 64])
    o_sgv = P.dout("o_sgv", [NSQ * LS, D])

    st = contextlib.ExitStack()
    RSIZE = 33500
    BSIZE = 20600
    FSIZE = 53000 - RSIZE // 2 - BSIZE // 2
    arF_t = st.enter_context(nc.sbuf_tensor("arenaF", [128, FSIZE], F32))
    arR_t = st.enter_context(nc.sbuf_tensor("arenaR", [128, RSIZE], BF16))
    NWB = 43
    wscr = nc.dram_tensor("wscr", [NWB, 128, 4096], BF16).ap()
    arB_t = st.enter_context(nc.sbuf_tensor("arenaB", [128, BSIZE], BF16))
    ABr = Arena("B", arB_t[:, :], BSIZE)
    ba = ABr.alloc
    ps_t = st.enter_context(nc.psum_tensor("psum", [128, 4096], F32))
    AFr = Arena("F", arF_t[:, :], FSIZE)
    ARr = Arena("R", arR_t[:, :], RSIZE)
    P.PS = Arena("P", ps_t[:, :], 4096)
    fa = AFr.alloc
    ra = ARr.alloc

    cst = fa(NCONST)
    P.dma("sp", cst.ap(), consts, writes=[cst])

    def cbuf(name):
        o, n = COFF[name]
        return cst.sub(o, n)

    ident = cbuf("ident")
    blk64 = cbuf("blk64")
    onesR = ra(128)
    epsb = fa(8)
    for i, v in enumerate([RMS_EPS, LN_EPS, GN_EPS, 1e-12, 0.0]):
        P.op("dve", lambda e, i=i, v=v: e.memset(epsb.ap(c0=i, c1=i + 1), v), writes=[epsb])

    P.tick = lambda e: e.memset(epsb.ap(c0=7, c1=8), 0.0)
    P.op("pool", lambda e: e.memset(onesR.ap(), 1.0), writes=[onesR])

    def eps_ap(i, p0=0, p1=128):
        return epsb.ap(p0, p1, i, i + 1)

    vrows = fa(128)
    P.op("pool", lambda e: e.memset(vrows.ap(), 0.0), writes=[vrows])
    for i, n in enumerate(VEC_NAMES):
        P.dma("sp", vrows.ap(8 * i, 8 * i + 8), vec[n].rearrange("(k p) -> k p", p=128), writes=[vrows])
    murows = fa(128)
    P.op("pool", lambda e: e.memset(murows.ap(), 0.0), writes=[murows])
    P.dma("sp", murows.ap(0, NSC), shift_mu.rearrange("(k p) -> k p", p=128), writes=[murows])
    vcols = fa(128)
    mucols = fa(32)
    bk = P.bank()
    P.tr(bk.ap(c0=0, c1=128), vrows.ap(), ident.ap(), [vrows, ident], [bk])
    P.tr(bk.ap(c0=128, c1=256), murows.ap(), ident.ap(), [murows, ident], [bk])
    P.copy("dve", vcols.ap(), bk.ap(c0=0, c1=128), [bk], [vcols])
    P.copy("dve", mucols.ap(), bk.ap(c0=128, c1=160), [bk], [mucols])

    def vc(name):
        i = VEC_NAMES.index(name)
        return vcols.sub(8 * i, 8)

    omka = fa(8)
    P.ts("dve", omka.ap(), vc("k_a").ap(), -1.0, 1.0, ALU.mult, ALU.add, [vcols], [omka])
    lw12 = fa(1024)
    P.dma("sp", lw12.ap(0, 64), w_w2, writes=[lw12])
    P.dma("sp", lw12.ap(64, 128), w_a2, writes=[lw12])
    lwg = fa(1024)
    P.dma("sp", lwg.ap(), w_g2, writes=[lwg])
    markF_sample = AFr.mark()
    carry = fa(NSC)
    P.op("dve", lambda e: e.memset(carry.ap(), 0.0), writes=[carry])
    STp = fa(512)
    P.op("dve", lambda e: e.memset(STp.ap(), 0.0), writes=[STp])
    Vm = fa(2 * D)
    KTm = ra(KC * NMEM)
    NRING = 6
    wbufs = [ra(4096) for _ in range(NRING)]

    wstate = {"i": 0}
    wblocks = {}
    wconverted = set()

    def wfetch(key, kcn, ncols):
        ring = wstate["i"] % NRING
        b = wbufs[ring]
        wstate["i"] += 1
        wb = b.sub(0, kcn * ncols)
        if isinstance(key, tuple) and key[0] == "direct":
            P.dma("pool", wb.v3(ncols), key[1], writes=[b])
        else:
            bi = wblocks[key]
            dreg = Reg("D", [(bi * GR, 1)])
            if key not in wconverted:
                wconverted.add(key)
                wname, r0, c0, nco = key
                P.dma("pool", wb.v3(ncols), src3(wtensors[wname], r0, c0, nco, kcn), writes=[b])
                P.dma("sp", wscr[bi][:, 0:kcn * ncols], wb.ap(), reads=[b], writes=[dreg])
            else:
                P.dma("sp", wb.ap(), wscr[bi][:, 0:kcn * ncols], reads=[dreg], writes=[b])
        return wb

    wtensors = {"w_in": w_in, "w_out": w_out, "w_up": w_up, "w_down": w_down,
                "w_branch0": w_branch[0], "w_branch1": w_branch[1], "w_branch2": w_branch[2]}

    def src3(w2d, r0, c0, ncols, kcn=8):
        return w2d[r0:r0 + 128 * kcn, c0:c0 + ncols].rearrange("(kc p) n -> p kc n", p=128)

    def wsrc(wname, r0, c0, ncols, kcn=8):
        key = (wname, r0, c0, ncols)
        if key not in wblocks:
            wblocks[key] = len(wblocks)
        return (key, kcn, ncols)

    def wsrc_direct(w2d, r0, c0, ncols, kcn=8):
        return (("direct", src3(w2d, r0, c0, ncols, kcn)), kcn, ncols)

    def convert_weights():
        for key, bi in sorted(wblocks.items(), key=lambda kv: kv[1]):
            wname, r0, c0, ncols = key
            ring = wstate["i"] % NRING
            b = wbufs[ring]
            wstate["i"] += 1
            wb = b.sub(0, 8 * ncols)
            P.dma("pool", wb.v3(ncols), src3(wtensors[wname], r0, c0, ncols), writes=[b])
            P.dma("sp", wscr[bi][:, 0:8 * ncols], wb.ap(), reads=[b], writes=[Reg("D", [(bi * GR, 1)])])

    class Jobs:
        def __init__(self):
            self.jobs = []

        def add(self, src, fn):
            self.jobs.append((src, fn))

        def run(self):
            jobs = self.jobs
            n = len(jobs)
            fetched = {}
            wl_ = [i for i in range(n) if jobs[i][0] is not None]
            for i in range(n):
                if jobs[i][0] is not None and i not in fetched:
                    fetched[i] = wfetch(*jobs[i][0])
                for j in wl_:
                    if j > i and j not in fetched and len(fetched) < NRING:
                        fetched[j] = wfetch(*jobs[j][0])
                jobs[i][1](fetched.pop(i, None))

    def linear_fm(wb, ncols, actT, NT, sink, oc0=0, kcn=8):
        for oc in range(ncols // 128):
            bk = P.bank()
            for kc in range(kcn):
                P.mm(bk.ap(c1=NT), wb.ap(c0=kc * ncols + oc * 128, c1=kc * ncols + oc * 128 + 128),
                     actT.ap(c0=kc * NT, c1=(kc + 1) * NT), kc == 0, kc == kcn - 1,
                     [wb, actT.sub(kc * NT, NT)], [bk])
            sink(oc0 + oc, bk)

    def stage_x(src2d, NT, xtok):
        if NT >= 128:
            P.dma("sp", xtok.v3(D), src2d.rearrange("(t p) d -> p t d", p=128), writes=[xtok])
        else:
            P.dma("sp", xtok.ap(0, min(128, NT)), src2d, writes=[xtok])

    def load_xT(src2d, NT, xT, stage=None):
        m = AFr.mark()
        ntl = max(1, NT // 128)
        pt = min(128, NT)
        if stage is None:
            xtok = fa(ntl * D)
            stage_x(src2d, NT, xtok)
        else:
            xtok = stage
        for kc in range(KC):
            bk = P.bank()
            for ti in range(ntl):
                P.tr(bk.ap(c0=ti * 128, c1=ti * 128 + pt), xtok.ap(0, pt, ti * D + kc * 128, ti * D + kc * 128 + 128),
                     ident.ap(0, pt, 0, pt), [xtok, ident], [bk])
            P.copy(P.ev(), xT.ap(c0=kc * NT, c1=(kc + 1) * NT), bk.ap(c1=NT), [bk], [xT.sub(kc * NT, NT)])
        AFr.release(m)

    def rmsnorm_fm(xT, gcol, NT, hT):
        mF = AFr.mark()
        mR = ARr.mark()
        sq = [ra(NT), ra(NT)]
        ssb = P.bank()
        for kc in range(KC):
            s = sq[kc % 2]
            P.act(s.ap(), xT.ap(c0=kc * NT, c1=(kc + 1) * NT), AF.Square, [xT.sub(kc * NT, NT)], [s])
            P.mm(ssb.ap(c1=NT), onesR.ap(), s.ap(), kc == 0, kc == KC - 1, [onesR, s], [ssb])
        rstd = fa(NT)
        P.act(rstd.ap(), ssb.ap(c1=NT), AF.Sqrt, [ssb, epsb], [rstd], bias=eps_ap(0), scale=1.0 / D)
        P.op("dve", lambda e: e.reciprocal(out=rstd.ap(), in_=rstd.ap()), [rstd], [rstd])
        for kc in range(KC):
            P.stt("dve" if kc % 2 else "pool", hT.ap(c0=kc * NT, c1=(kc + 1) * NT), xT.ap(c0=kc * NT, c1=(kc + 1) * NT),
                  gcol.ap(c0=kc, c1=kc + 1), rstd.ap(), ALU.mult, ALU.mult,
                  [xT.sub(kc * NT, NT), gcol, rstd], [hT.sub(kc * NT, NT)])
        AFr.release(mF)
        ARr.release(mR)

    def add_mem_jobs(jobs, hmT):
        st_ = {}

        def begin(_):
            st_["m"] = AFr.mark()
            mT = fa(KC * NMEM)
            load_xT(memp, NMEM, mT)
            rmsnorm_fm(mT, vc("mem_norm_g"), NMEM, hmT)
            st_["stage"] = fa(2 * D)

        def kjob(half):
            def f(wb):
                stage = st_["stage"]

                def sink(oc, bk):
                    P.copy(P.ev(), KTm.ap(c0=oc * NMEM, c1=(oc + 1) * NMEM), bk.ap(c1=NMEM), [bk],
                           [KTm.sub(oc * NMEM, NMEM)])
                linear_fm(wb, 512, hmT, NMEM, sink, oc0=half * 4)
                for mt in range(2):
                    bk = P.bank()
                    for kc in range(KC):
                        P.mm(bk.ap(), hmT.ap(c0=kc * NMEM + mt * 128, c1=kc * NMEM + mt * 128 + 128),
                             wb.ap(c0=kc * 512, c1=(kc + 1) * 512), kc == 0, kc == KC - 1, [hmT, wb], [bk])
                    dst = stage.sub(mt * D + half * 512, 512)
                    P.copy(P.ev(), dst.ap(), bk.ap(), [bk], [dst])
            return f

        def vjob(half):
            def f(wb):
                for mt in range(2):
                    bk = P.bank()
                    for kc in range(KC):
                        P.mm(bk.ap(), hmT.ap(c0=kc * NMEM + mt * 128, c1=kc * NMEM + mt * 128 + 128),
                             wb.ap(c0=kc * 512, c1=(kc + 1) * 512), kc == 0, kc == KC - 1, [hmT, wb], [bk])
                    dst = Vm.sub(mt * D + half * 512, 512)
                    P.copy(P.ev(), dst.ap(), bk.ap(), [bk], [dst])
            return f

        def end(_):
            P.dma("sp", o_pmv.rearrange("(t p) d -> p t d", p=128), Vm.v3(D), reads=[Vm])
            AFr.release(st_["m"])

        jobs.add(None, begin)
        for half in range(2):
            jobs.add(wsrc_direct(w_mem_k, 0, half * 512, 512), kjob(half))
        jobs.add(None, lambda wb: P.dma("sp", o_pmk.rearrange("(t p) d -> p t d", p=128), st_["stage"].v3(D),
                                        reads=[st_["stage"]]))
        for half in range(2):
            jobs.add(wsrc_direct(w_mem_v, 0, half * 512, 512), vjob(half))
        jobs.add(None, end)

    def rwkv_group(prompt, NT, nseq, L, C, ss, yT, g, sso, last):
        m0 = AFr.mark()
        mB0 = ABr.mark()
        CW = 8 * C
        nch = NT // C
        nlev = 7 if prompt else 2
        mS = cbuf("mS4") if prompt else cbuf("mS4s")
        mI = cbuf("mI4") if prompt else cbuf("mI4s")
        mST = cbuf("mST4") if prompt else cbuf("mST4s")
        segm = cbuf("segP") if prompt else cbuf("segS")
        rowmask = cbuf("rowmask")
        W4 = 4 * C
        XW = 4 * 64

        def bc8(colbuf):
            return colbuf.pat([[1, 8], [0, C]])

        if prompt:
            STs = None
            STb = ba(512)
            P.copy("pool", STb.ap(), STp.ap(), [STp], [STb])
        else:
            STs = fa(nseq * 512)
            s0 = [fa(1024), fa(1024)]
            for b in range(nseq):
                sb_ = s0[b % 2]
                P.dma("sp", sb_.ap(0, 64).rearrange("p (h j) -> p h j", j=64), swkv[b].rearrange("h i j -> i h j"),
                      writes=[sb_])
                bk = P.bank()
                for kc in range(8):
                    P.tr(bk.ap(c0=kc * 64, c1=kc * 64 + 64), sb_.ap(0, 64, kc * 128, kc * 128 + 128),
                         ident.ap(0, 64, 0, 64), [sb_, ident], [bk])
                dst = STs.sub(b * 512, 512)
                P.copy(P.ev(), dst.ap(), bk.ap(), [bk], [dst])
            STb = ba(nseq * 128)
            Apad = [ba(nseq * C), ba(nseq * C)]
            Rpad = [ba(nseq * C), ba(nseq * C)]
            for pb in Apad + Rpad:
                P.op("pool", lambda e, pb=pb: e.memset(pb.ap(), 0.0), [], [pb])
            tmpBs = [ba(1024), ba(1024)]
            tmpKs = [ba(1024), ba(1024)]

        TW = max(CW, 1024)
        t0, t1, t2, t3 = fa(CW), fa(CW), fa(CW), fa(CW)
        t4f, t5f = fa(TW), fa(TW)
        t4, t5 = t4f.sub(0, CW), t5f.sub(0, CW)
        t10, t11 = fa(CW), fa(TW)
        t11c = t11.sub(0, CW)
        Bt, Kt = ba(CW), ba(CW)
        AtP = [ba(CW), ba(CW)]
        RtP = [ba(CW), ba(CW)]
        for pb in AtP + RtP:
            P.op("pool", lambda e, pb=pb: e.memset(pb.ap(), 0.0), [], [pb])
        Vtok, BhT, KhT, Utok = ba(1024), ba(1024), ba(1024), ba(1024)
        Ytok = t4f
        lor = fa(2 * C)
        wc = fa(8 * nseq)
        gst = fa(64)
        nsets = 2 if prompt else 1
        MB = [dict(Pm=[ba(W4), ba(W4)], PTm=[ba(W4), ba(W4)], Tm=[ba(W4), ba(W4)], mak=ba(W4), mbr=ba(W4), mkr=ba(W4),
                   xts=ba(XW)) for _ in range(nsets)]
        hb_groups = [(0, 1), (2, 3)] if prompt else [(0,), (1,), (2,), (3,)]
        ss3 = ss.v3(NT)

        for ci in range(nch):
            c0 = ci * C

            def sl(sc0, p0=0, p1=128):
                return ss.v3(NT, p0, p1)[:, sc0:sc0 + 8, c0:c0 + C]

            def slr(sc0, n=8):
                return cols_reg(ss.sub(sc0 * NT, n * NT), NT, c0, C, n)
            r3, k3, v3 = sl(0), sl(8), sl(16)
            rR, kR, vR = slr(0), slr(8), slr(16)
            loraR = slr(24, 2)
            wl = ss.v3(NT, 0, 64)[:, 24, c0:c0 + C]
            al = ss.v3(NT, 64, 128)[:, 24, c0:c0 + C]
            gl = ss3[:, 25, c0:c0 + C]

            def f3(buf, p0=0, p1=128):
                return buf.v3(C, p0, p1)

            P.act(lor.ap(0, 64, 0, C), wl, AF.Tanh, [loraR], [lor])
            P.act(lor.ap(0, 128, C, 2 * C), gl, AF.Sigmoid, [loraR], [lor])
            P.tt("pool", f3(t5), k3, bc8(vc("k_k")), ALU.mult, [kR, vcols], [t5])
            bkt = P.bank2()
            for kc in range(8):
                P.tr(bkt.ap(0, C, kc * 128, kc * 128 + 128), v3[:, kc, :], ident.ap(), [vR, ident], [bkt])
            for half in range(2):
                P.copy("dve" if half else "act", Vtok.ap(0, C, half * 512, half * 512 + 512),
                       bkt.ap(0, C, half * 512, half * 512 + 512), [bkt], [Vtok.sub(half * 512, 512)])
            bka = P.banks_for(CW)
            for kc in range(8):
                P.mm(bka.ap(c0=kc * C, c1=(kc + 1) * C), lw12.ap(64, 128, kc * 128, kc * 128 + 128), al,
                     True, True, [lw12, loraR], [bka])
            bkw = P.banks_for(CW)
            for kc in range(8):
                P.mm(bkw.ap(c0=kc * C, c1=(kc + 1) * C), lw12.ap(0, 64, kc * 128, kc * 128 + 128), lor.ap(0, 64, 0, C),
                     True, True, [lw12, lor], [bkw])
            P.act(t4.ap(), t5.ap(), AF.Square, [t5], [t4])
            for kc in range(8):
                P.act(t0.ap(c0=kc * C, c1=(kc + 1) * C), bkw.ap(c0=kc * C, c1=(kc + 1) * C), AF.Sigmoid,
                      [bkw, vcols], [t0.sub(kc * C, C)], bias=vc("w0").ap(c0=kc, c1=kc + 1), scale=1.0)
            bkk = P.banks_for(CW)
            for kc in range(8):
                P.mm(bkk.ap(c0=kc * C, c1=(kc + 1) * C), blk64.ap(), t4.ap(c0=kc * C, c1=(kc + 1) * C), True, True,
                     [blk64, t4.sub(kc * C, C)], [bkk])
            P.ts("dve", t0.ap(), t0.ap(), -EXPM05, None, ALU.mult, None, [t0], [t0])
            for kc in range(8):
                P.act(t10.ap(c0=kc * C, c1=(kc + 1) * C), bka.ap(c0=kc * C, c1=(kc + 1) * C), AF.Sigmoid,
                      [bka, vcols], [t10.sub(kc * C, C)], bias=vc("a0").ap(c0=kc, c1=kc + 1), scale=1.0)
            P.op("dve", lambda e: e.tensor_tensor_scan(out=t1.ap(), data0=segm.ap(c1=CW), data1=t0.ap(),
                                                       initial=0.0, op0=ALU.mult, op1=ALU.add),
                 [segm, t0], [t1])
            P.act(t4.ap(), bkk.ap(c1=CW), AF.Sqrt, [bkk, epsb], [t4], bias=eps_ap(3), scale=1.0)
            P.tt("dve", t0.ap(), t1.ap(), t0.ap(), ALU.subtract, [t0, t1], [t0])
            P.op("dve", lambda e: e.reciprocal(out=t4.ap(), in_=t4.ap()), [t4], [t4])
            P.act(t2.ap(), t1.ap(), AF.Exp, [t1], [t2])
            P.tt("dve", t5.ap(), t5.ap(), t4.ap(), ALU.mult, [t5, t4], [t5])
            P.act(t3.ap(), t1.ap(), AF.Exp, [t1], [t3], scale=-1.0)
            P.act(t0.ap(), t0.ap(), AF.Exp, [t0], [t0])
            for kc in range(8):
                P.act(t4.ap(c0=kc * C, c1=(kc + 1) * C), t10.ap(c0=kc * C, c1=(kc + 1) * C), AF.Identity,
                      [t10.sub(kc * C, C), vcols, omka], [t4.sub(kc * C, C)], bias=omka.ap(c0=kc, c1=kc + 1),
                      scale=vc("k_a").ap(c0=kc, c1=kc + 1))
            P.tt("dve", f3(t4), f3(t4), k3, ALU.mult, [t4, kR], [t4])
            if prompt:
                P.copy("pool", wc.ap(), t2.pat([[C, 8]], off=C - 1), [t2], [wc])
            else:
                P.copy("pool", wc.v3(nseq), t2.pat([[C, 8], [L, nseq]], off=L - 1), [t2], [wc])
            for par in range(2):
                p0, p1 = par * 64, par * 64 + 64
                P.stt("dve", AtP[par].ap(p0, p1), t5.ap(p0, p1), -1.0, t0.ap(p0, p1), ALU.mult, ALU.mult, [t5, t0], [AtP[par]])
                P.tt("pool", f3(RtP[par], p0, p1), sl(0, p0, p1), f3(t2, p0, p1), ALU.mult, [rR, t2], [RtP[par]])
            P.tt("dve", t0.ap(), t5.ap(), t10.ap(), ALU.mult, [t5, t10], [t0])
            for kc in range(8):
                P.act(t1.ap(c0=kc * C, c1=(kc + 1) * C), r3[:, kc, :], AF.Copy, [rR, vcols], [t1.sub(kc * C, C)],
                      scale=vc("r_k").ap(c0=kc, c1=kc + 1))
            P.tt("dve", t1.ap(), t1.ap(), t4.ap(), ALU.mult, [t1, t4], [t1])
            bkb = P.banks_for(CW)
            for kc in range(8):
                P.mm(bkb.ap(c0=kc * C, c1=(kc + 1) * C), blk64.ap(), t1.ap(c0=kc * C, c1=(kc + 1) * C), True, True,
                     [blk64, t1.sub(kc * C, C)], [bkb])
            P.tt("dve", t11c.ap(), t0.ap(), t3.ap(), ALU.mult, [t0, t3], [t11c])
            P.tt("pool", t5.ap(), t4.ap(), t3.ap(), ALU.mult, [t4, t3], [t5])
            P.copy("act", Bt.ap(), t11c.ap(), [t11c], [Bt])
            P.copy("act", Kt.ap(), t5.ap(), [t5], [Kt])
            P.tt("dve", f3(t1), bkb.v3(C)[:, 0:8, :], v3, ALU.mult, [bkb, vR], [t1])
            if prompt:
                wcb = t2.pat([[C, 8], [0, C]], off=C - 1)
                P.tt("dve", f3(t11c), f3(t11c), wcb, ALU.mult, [t11c, t2], [t11c])
                P.tt("pool", f3(t5), f3(t5), wcb, ALU.mult, [t5, t2], [t5])
            else:
                wcb = t2.pat([[C, 8], [L, nseq], [0, L]], off=L - 1)
                P.tt("dve", t11c.v4(nseq, L), t11c.v4(nseq, L), wcb, ALU.mult, [t11c, t2], [t11c])
                P.tt("pool", t5.v4(nseq, L), t5.v4(nseq, L), wcb, ALU.mult, [t5, t2], [t5])
            def hq(hb, q):
                return 2 * hb + q // 2, (q % 2) * 64

            for hbs in hb_groups:
                cur = {}
                for hb in hbs:
                    S_ = MB[hb % len(MB)]
                    bA, bAT, bK, bBR, bKR = P.bank(), P.bank(), P.bank(), P.bank(), P.bank()
                    for q in range(4):
                        kc, pr = hq(hb, q)
                        At, Rt = AtP[q % 2], RtP[q % 2]
                        Bq = Bt.ap(c0=kc * C, c1=(kc + 1) * C)
                        Kq = Kt.ap(c0=kc * C, c1=(kc + 1) * C)
                        Aq = At.ap(c0=kc * C, c1=(kc + 1) * C)
                        Rq = Rt.ap(c0=kc * C, c1=(kc + 1) * C)
                        o = lambda bk_: bk_.ap(0, C, q * C, (q + 1) * C)
                        P.mm(o(bA), Bq, Aq, True, True, [Bt, At], [bA])
                        P.mm(o(bAT), Aq, Bq, True, True, [Bt, At], [bAT])
                        P.mm(o(bK), Kq, Aq, True, True, [Kt, At], [bK])
                        P.mm(o(bBR), Bq, Rq, True, True, [Bt, Rt], [bBR])
                        P.mm(o(bKR), Kq, Rq, True, True, [Kt, Rt], [bKR])
                    P.tt("dve", S_["Pm"][0].ap(0, C), bA.ap(0, C, 0, W4), mS.ap(0, C, 0, W4), ALU.mult, [bA, mS], [S_["Pm"][0]])
                    P.tt("dve", S_["PTm"][0].ap(0, C), bAT.ap(0, C, 0, W4), mST.ap(0, C, 0, W4), ALU.mult, [bAT, mST],
                         [S_["PTm"][0]])
                    P.tt("dve", S_["mak"].ap(0, C), bK.ap(0, C, 0, W4), mS.ap(0, C, 0, W4), ALU.mult, [bK, mS], [S_["mak"]])
                    P.tt("dve", S_["mbr"].ap(0, C), bBR.ap(0, C, 0, W4), mI.ap(0, C, 0, W4), ALU.mult, [bBR, mI], [S_["mbr"]])
                    P.tt("dve", S_["mkr"].ap(0, C), bKR.ap(0, C, 0, W4), mI.ap(0, C, 0, W4), ALU.mult, [bKR, mI], [S_["mkr"]])
                    P.tt("pool", S_["Tm"][0].v3(C, 0, C), S_["Pm"][0].v3(C, 0, C), ident.pat([[0, 4], [1, C]], p0=0, p1=C),
                         ALU.add, [S_["Pm"][0], ident], [S_["Tm"][0]])
                    cur[hb] = 0
                for lev in range(1, nlev):
                    lastl = lev == nlev - 1
                    bks = {}
                    for hb in hbs:
                        S_ = MB[hb % len(MB)]
                        c_ = cur[hb]
                        bP = None if lastl else P.bank()
                        bPT = P.bank()
                        bks[hb] = (bP, bPT)
                        for q in range(4):
                            Pq = S_["Pm"][c_].ap(0, C, q * C, (q + 1) * C)
                            PTq = S_["PTm"][c_].ap(0, C, q * C, (q + 1) * C)
                            P.mm(bPT.ap(0, C, q * C, (q + 1) * C), Pq, PTq, True, True, [S_["Pm"][c_], S_["PTm"][c_]], [bPT])
                        for q in range(4):
                            Pq = S_["Pm"][c_].ap(0, C, q * C, (q + 1) * C)
                            PTq = S_["PTm"][c_].ap(0, C, q * C, (q + 1) * C)
                            if not lastl:
                                P.mm(bP.ap(0, C, q * C, (q + 1) * C), PTq, Pq, True, True, [S_["Pm"][c_], S_["PTm"][c_]], [bP])
                    for hb in hbs:
                        S_ = MB[hb % len(MB)]
                        nx = cur[hb] ^ 1
                        bP, bPT = bks[hb]
                        P.copy("act", S_["PTm"][nx].ap(0, C), bPT.ap(0, C, 0, W4), [bPT], [S_["PTm"][nx]])
                        if not lastl:
                            P.copy("dve", S_["Pm"][nx].ap(0, C), bP.ap(0, C, 0, W4), [bP], [S_["Pm"][nx]])
                    bts = {}
                    for hb in hbs:
                        S_ = MB[hb % len(MB)]
                        c_ = cur[hb]
                        nx = c_ ^ 1
                        bT = P.bank()
                        bts[hb] = bT
                        for q in range(4):
                            P.mm(bT.ap(0, C, q * C, (q + 1) * C), S_["PTm"][nx].ap(0, C, q * C, (q + 1) * C),
                                 S_["Tm"][c_].ap(0, C, q * C, (q + 1) * C), True, True, [S_["PTm"][nx], S_["Tm"][c_]], [bT])
                    for hb in hbs:
                        S_ = MB[hb % len(MB)]
                        c_ = cur[hb]
                        nx = c_ ^ 1
                        P.tt("dve", S_["Tm"][nx].ap(0, C), S_["Tm"][c_].ap(0, C), bts[hb].ap(0, C, 0, W4), ALU.add,
                             [S_["Tm"][c_], bts[hb]], [S_["Tm"][nx]])
                        cur[hb] = nx
                bxs = {}
                for hb in hbs:
                    S_ = MB[hb % len(MB)]
                    mak_ = S_["mak"]
                    if not prompt:
                        P.copy("pool", STb.v4(2, 64), STs.v4(8, 64)[:, :, 2 * hb:2 * hb + 2, :], [STs], [STb])
                    bX = P.bank()
                    bX2 = None if prompt else P.bank()
                    bxs[hb] = (bX, bX2)
                    for q in range(4):
                        kc, pr = hq(hb, q)
                        h = 4 * hb + q
                        ox = bX.ap(0, C, q * 64, q * 64 + 64)
                        At = AtP[q % 2]
                        if prompt:
                            P.mm(ox, At.ap(c0=kc * C, c1=(kc + 1) * C), STb.ap(c0=kc * 64, c1=kc * 64 + 64),
                                 True, False, [At, STb], [bX])
                            P.mm(ox, mak_.ap(0, C, q * C, (q + 1) * C), Vtok.ap(0, C, h * 64, h * 64 + 64), False, True,
                                 [mak_, Vtok], [bX])
                        else:
                            ap_ = Apad[q % 2]
                            P.copy("pool", ap_.pat([[C + L, nseq], [1, L]], p0=pr, p1=pr + 64),
                                   At.ap(pr, pr + 64, kc * C, (kc + 1) * C).rearrange("p (b l) -> p b l", l=L), [At, ap_], [ap_])
                            for b in range(nseq):
                                o_ = b * 128 + (q // 2) * 64
                                P.mm(ox, ap_.ap(c0=b * C, c1=(b + 1) * C), STb.ap(c0=o_, c1=o_ + 64),
                                     b == 0, b == nseq - 1, [ap_, STb], [bX])
                            P.mm(bX2.ap(0, C, q * 64, q * 64 + 64), mak_.ap(0, C, q * C, (q + 1) * C),
                                 Vtok.ap(0, C, h * 64, h * 64 + 64), True, True, [mak_, Vtok], [bX2])
                for hb in hbs:
                    S_ = MB[hb % len(MB)]
                    bX, bX2 = bxs[hb]
                    xts = S_["xts"]
                    if prompt:
                        P.copy("act", xts.ap(0, C), bX.ap(0, C, 0, XW), [bX], [xts])
                    else:
                        ys = Ytok.sub(hb * XW, XW)
                        P.copy("act", ys.ap(0, C), bX.ap(0, C, 0, XW), [bX], [ys])
                        P.tt("dve", xts.ap(0, C), ys.ap(0, C), bX2.ap(0, C, 0, XW), ALU.add, [ys, bX2], [xts])
                bus = {}
                for hb in hbs:
                    S_ = MB[hb % len(MB)]
                    Tf = S_["Tm"][cur[hb]]
                    xts = S_["xts"]
                    bU = P.bank()
                    bus[hb] = bU
                    for q in range(4):
                        P.mm(bU.ap(0, C, q * 64, q * 64 + 64), Tf.ap(0, C, q * C, (q + 1) * C), xts.ap(0, C, q * 64, q * 64 + 64),
                             True, True, [Tf, xts], [bU])
                for hb in hbs:
                    ud = Utok.sub(hb * XW, XW)
                    P.copy("act", ud.ap(0, C), bus[hb].ap(0, C, 0, XW), [bus[hb]], [ud])
                bys = {}
                for hb in hbs:
                    S_ = MB[hb % len(MB)]
                    mbr_, mkr_ = S_["mbr"], S_["mkr"]
                    ud = Utok.sub(hb * XW, XW)
                    bY = P.bank()
                    bY2 = None if prompt else P.bank()
                    bys[hb] = (bY, bY2)
                    for q in range(4):
                        kc, pr = hq(hb, q)
                        h = 4 * hb + q
                        oy = bY.ap(0, C, q * 64, q * 64 + 64)
                        Rt = RtP[q % 2]
                        if prompt:
                            P.mm(oy, Rt.ap(c0=kc * C, c1=(kc + 1) * C), STb.ap(c0=kc * 64, c1=kc * 64 + 64),
                                 True, False, [Rt, STb], [bY])
                            oy2, by2, st2 = oy, bY, False
                        else:
                            rp_ = Rpad[q % 2]
                            P.copy("pool", rp_.pat([[C + L, nseq], [1, L]], p0=pr, p1=pr + 64),
                                   Rt.ap(pr, pr + 64, kc * C, (kc + 1) * C).rearrange("p (b l) -> p b l", l=L), [Rt, rp_], [rp_])
                            for b in range(nseq):
                                o_ = b * 128 + (q // 2) * 64
                                P.mm(oy, rp_.ap(c0=b * C, c1=(b + 1) * C), STb.ap(c0=o_, c1=o_ + 64),
                                     b == 0, b == nseq - 1, [rp_, STb], [bY])
                            oy2, by2, st2 = bY2.ap(0, C, q * 64, q * 64 + 64), bY2, True
                        P.mm(oy2, mbr_.ap(0, C, q * C, (q + 1) * C), Utok.ap(0, C, h * 64, h * 64 + 64), st2, False,
                             [mbr_, ud], [by2])
                        P.mm(oy2, mkr_.ap(0, C, q * C, (q + 1) * C), Vtok.ap(0, C, h * 64, h * 64 + 64), False, True,
                             [mkr_, Vtok], [by2])
                for hb in hbs:
                    bY, bY2 = bys[hb]
                    yd = Ytok.sub(hb * XW, XW)
                    P.copy("dve", yd.ap(0, C), bY.ap(0, C, 0, XW), [bY], [yd])
                    if not prompt:
                        P.tt("dve", yd.ap(0, C), yd.ap(0, C), bY2.ap(0, C, 0, XW), ALU.add, [yd, bY2], [yd])

            for srcb, src3, srcR, dst in ((t11c, None, t11c, BhT), (t5, None, t5, KhT)):
                bkt = P.bank2()
                for kc in range(8):
                    inp = src3[:, kc, :] if srcb is None else srcb.ap(c0=kc * C, c1=(kc + 1) * C)
                    P.tr(bkt.ap(0, C, kc * 128, kc * 128 + 128), inp, ident.ap(), [srcR, ident], [bkt])
                for half in range(2):
                    P.copy(P.ev(), dst.ap(0, C, half * 512, half * 512 + 512), bkt.ap(0, C, half * 512, half * 512 + 512),
                           [bkt], [dst.sub(half * 512, 512)])

            if not prompt:
                P.tt("dve", STs.v4(8, 64), STs.v4(8, 64), wc.pat([[1, nseq], [nseq, 8], [0, 64]]), ALU.mult, [STs, wc], [STs])
            def masks(b):
                P.act(tmpBs[b % 2].ap(0, C), BhT.ap(0, C), AF.Copy, [BhT, rowmask], [tmpBs[b % 2]],
                      scale=rowmask.ap(0, C, b, b + 1))
                P.act(tmpKs[b % 2].ap(0, C), KhT.ap(0, C), AF.Copy, [KhT, rowmask], [tmpKs[b % 2]],
                      scale=rowmask.ap(0, C, b, b + 1))
            if not prompt:
                masks(0)
            for b in range(nseq):
                if prompt:
                    lB, lK, STm = BhT, KhT, STp
                else:
                    lB, lK, STm = tmpBs[b % 2], tmpKs[b % 2], STs.sub(b * 512, 512)
                bS = P.bank2()
                for kc in range(8):
                    osl = bS.ap(c0=kc * 128, c1=kc * 128 + 128)
                    P.mm(osl, lB.ap(0, C, kc * 128, kc * 128 + 128), Utok.ap(0, C, kc * 128, kc * 128 + 128), True, False,
                         [lB, Utok], [bS])
                    P.mm(osl, lK.ap(0, C, kc * 128, kc * 128 + 128), Vtok.ap(0, C, kc * 128, kc * 128 + 128), False, True,
                         [lK, Vtok], [bS])
                for half in range(2):
                    p0, p1 = half * 64, half * 64 + 64
                    s3 = STm.v3(64, p0, p1)
                    if prompt:
                        wbc = wc.pat([[1, 8], [0, 64]], p0=p0, p1=p1)
                        P.tt("dve", s3, s3, wbc, ALU.mult, [STm, wc], [STm])
                    P.tt("dve", s3, s3, bS.v3(128, p0, p1)[:, :, half * 64:half * 64 + 64], ALU.add, [STm, bS], [STm])
                if prompt:
                    P.copy("pool", STb.ap(), STp.ap(), [STp], [STb])
                elif b + 1 < nseq:
                    masks(b + 1)
                if (prompt and last and ci == nch - 1) or (not prompt):
                    bo = P.bank2()
                    for kc in range(8):
                        P.tr(bo.ap(0, 64, kc * 128, kc * 128 + 128), STm.ap(c0=kc * 64, c1=kc * 64 + 64), ident.ap(),
                             [STm, ident], [bo])
                    so_full = t11 if prompt else [t5f, t11][b % 2]
                    P.copy("act", so_full.ap(0, 64, 0, 1024), bo.ap(0, 64, 0, 1024), [bo], [so_full])
                    dsto = o_pwkv if prompt else o_swkv[b]
                    P.dma("sp", dsto.rearrange("h i j -> i h j"), so_full.ap(0, 64, 0, 1024).rearrange("p (h j) -> p h j", j=64),
                          reads=[so_full])

            y16 = Ytok.v3(64, 0, C)
            s1 = gst.sub(0, 16)
            P.op("dve", lambda e: e.tensor_reduce(out=s1.ap(0, C), in_=y16, axis=AX.X, op=ALU.add), [Ytok], [s1])
            P.ts("dve", s1.ap(0, C), s1.ap(0, C), -1.0 / 64, None, ALU.mult, None, [s1], [s1])
            P.tt("dve", y16, y16, s1.pat([[1, 16], [0, 64]], p0=0, p1=C), ALU.add, [Ytok, s1], [Ytok])
            junk = t5f
            P.act(junk.ap(0, C, 0, 1024), Ytok.ap(0, C, 0, 1024), AF.Square, [Ytok], [junk])
            s2 = gst.sub(16, 16)
            P.op("dve", lambda e: e.tensor_reduce(out=s2.ap(0, C), in_=junk.v3(64, 0, C)[:, 0:16, :], axis=AX.X, op=ALU.add),
                 [junk], [s2])
            P.act(s2.ap(0, C), s2.ap(0, C), AF.Sqrt, [s2, epsb], [s2], bias=eps_ap(2, 0, C), scale=1.0 / 64)
            P.op("dve", lambda e: e.reciprocal(out=s2.ap(0, C), in_=s2.ap(0, C)), [s2], [s2])
            P.tt("dve", y16, y16, s2.pat([[1, 16], [0, 64]], p0=0, p1=C), ALU.mult, [Ytok, s2], [Ytok])
            bkn = P.banks_for(CW)
            for kc in range(8):
                P.tr(bkn.ap(0, 128, kc * C, (kc + 1) * C), Ytok.ap(0, C, kc * 128, kc * 128 + 128), ident.ap(0, C, 0, C),
                     [Ytok, ident], [bkn])
            y1 = t0
            for kc in range(8):
                P.act(y1.ap(c0=kc * C, c1=(kc + 1) * C), bkn.ap(c0=kc * C, c1=(kc + 1) * C), AF.Identity,
                      [bkn, vcols], [y1.sub(kc * C, C)], bias=vc("lnx_b").ap(c0=kc, c1=kc + 1),
                      scale=vc("lnx_g").ap(c0=kc, c1=kc + 1))
            P.tt("pool", y1.ap(), y1.ap(), t1.ap(), ALU.add, [y1, t1], [y1])
            bkg = P.banks_for(CW)
            for kc in range(8):
                P.mm(bkg.ap(c0=kc * C, c1=(kc + 1) * C), lwg.ap(c0=kc * 128, c1=kc * 128 + 128), lor.ap(c0=C, c1=2 * C),
                     True, True, [lwg, lor], [bkg])
            yo3 = yT.v3(NT)[:, :, c0:c0 + C]
            yoR = Reg("R", [(yT.lo + k * NT + c0, C) for k in range(8)])
            P.tt("dve", yo3, f3(y1), bkg.v3(C)[:, 0:8, :], ALU.mult, [y1, bkg], [yoR])

        if prompt and last:
            bk = P.bank()
            P.tr(bk.ap(0, NSC, 0, 128), carry.ap(), ident.ap(), [carry, ident], [bk])
            so = t4f
            P.copy("act", so.ap(0, NSC, 0, 128), bk.ap(0, NSC, 0, 128), [bk], [so])
            P.dma("sp", o_pshift.rearrange("(k p) -> k p", p=128), so.ap(0, NSC, 0, 128), reads=[so])
        if not prompt:
            so = Buf(AFr, t0.lo, PSH)
            for sc in range(NSC):
                bk = P.bank()
                P.tr(bk.ap(0, nseq, 0, 128), sso.ap(c0=sc * nseq, c1=(sc + 1) * nseq), ident.ap(), [sso, ident], [bk])
                P.copy(P.ev(), so.ap(0, nseq, sc * 128, sc * 128 + 128), bk.ap(0, nseq, 0, 128), [bk], [so])
            P.dma("sp", o_sshift, so.ap(0, nseq), reads=[so])
        AFr.release(m0)
        ABr.release(mB0)

    def softmax_rows(bs, pt, pe, small):
        mx = small.sub(0, 4)
        P.op("dve", lambda e: e.tensor_reduce(out=mx.ap(0, pt), in_=bs.v3(NMEM, 0, pt), axis=AX.X, op=ALU.max),
             [bs], [mx])
        P.ts("dve", mx.ap(0, pt), mx.ap(0, pt), -1.0 / 16, None, ALU.mult, None, [mx], [mx])
        rs = small.sub(4, 4)
        for h in range(4):
            P.act(pe.ap(0, pt, h * NMEM, (h + 1) * NMEM), bs.ap(0, pt, h * NMEM, (h + 1) * NMEM), AF.Exp,
                  [bs, mx], [pe.sub(h * NMEM, NMEM), rs], bias=mx.ap(0, pt, h, h + 1), scale=1.0 / 16,
                  accum_out=rs.ap(0, pt, h, h + 1))
        P.op("dve", lambda e: e.reciprocal(out=rs.ap(0, pt), in_=rs.ap(0, pt)), [rs], [rs])
        P.tt("dve", pe.v3(NMEM, 0, pt), pe.v3(NMEM, 0, pt), rs.pat([[1, 4], [0, NMEM]], p0=0, p1=pt), ALU.mult,
             [pe, rs], [pe])

    def attention(prompt, NT, qT, yT, g):
        m0 = AFr.mark()
        mR = ARr.mark()
        mBa = ABr.mark()
        small = fa(8)
        pe = fa(4 * NMEM)
        if prompt:
            nt_ = NT // 128
            pT = fa(1024)
            small2 = fa(8)
            pes = [pe, F32View(Buf(ABr, 8192, 2048))]
            pTs = [pT, F32View(Buf(ABr, 10240, 2048))]
            smalls = [small, small2]
            bss = []
            for ti in range(nt_):
                cs = ti * 128
                bs = P.bank2()
                bss.append(bs)
                for h in range(4):
                    for k2 in range(2):
                        kc = 2 * h + k2
                        P.mm(bs.ap(c0=h * NMEM, c1=(h + 1) * NMEM), qT.ap(c0=kc * NT + cs, c1=kc * NT + cs + 128),
                             KTm.ap(c0=kc * NMEM, c1=(kc + 1) * NMEM), k2 == 0, k2 == 1, [qT, KTm], [bs])
            for ti in range(nt_):
                softmax_rows(bss[ti], 128, pes[ti % 2], smalls[ti % 2])
            for ti in range(nt_):
                pe_, pT_ = pes[ti % 2], pTs[ti % 2]
                bt = P.bank2()
                for h in range(4):
                    for mt in range(2):
                        o_ = (h * 2 + mt) * 128
                        P.tr(bt.ap(c0=o_, c1=o_ + 128), pe_.ap(c0=h * NMEM + mt * 128, c1=h * NMEM + mt * 128 + 128),
                             ident.ap(), [pe_, ident], [bt])
                for half in range(2):
                    P.copy(P.ev(), pT_.ap(c0=half * 512, c1=half * 512 + 512), bt.ap(c0=half * 512, c1=half * 512 + 512),
                           [bt], [pT_.sub(half * 512, 512)])
            for ti in range(nt_):
                cs = ti * 128
                pT_ = pTs[ti % 2]
                bo = P.bank2()
                for oc in range(8):
                    h = oc // 2
                    for mt in range(2):
                        o_ = (h * 2 + mt) * 128
                        P.mm(bo.ap(c0=oc * 128, c1=oc * 128 + 128), Vm.ap(c0=mt * D + oc * 128, c1=mt * D + oc * 128 + 128),
                             pT_.ap(c0=o_, c1=o_ + 128), mt == 0, mt == 1, [Vm, pT_], [bo])
                yR = Reg("R", [(yT.lo + k * NT + cs, 128) for k in range(8)])
                P.copy("act", yT.v3(NT)[:, :, cs:cs + 128], bo.v3(128), [bo], [yR])
        else:
            NTs = NSQ * LS
            qp = ra(KC * NTs)
            P.ts("pool", qp.ap(), qT.ap(), 0.0, None, ALU.mult, None, [qT], [qp])
            kbuf = [fa(2 * D) for _ in range(4)]
            KTb = [ra(KC * NMEM), ra(KC * NMEM)]
            sc = fa(4 * NMEM)
            P.op("pool", lambda e: e.memset(sc.ap(), 0.0), [], [sc])
            for b in range(NSQ):
                kb_ = kbuf[b % 4]
                kt = KTb[b % 2]
                P.dma("sp", kb_.v3(D), ck[b].rearrange("(t p) d -> p t d", p=128), writes=[kb_])
                for kc2 in range(4):
                    bk = P.bank()
                    for k2 in range(2):
                        kc = kc2 * 2 + k2
                        for mt in range(2):
                            P.tr(bk.ap(c0=k2 * NMEM + mt * 128, c1=k2 * NMEM + mt * 128 + 128),
                                 kb_.ap(c0=mt * D + kc * 128, c1=mt * D + kc * 128 + 128), ident.ap(), [kb_, ident], [bk])
                    dst = kt.sub(kc2 * 512, 512)
                    P.copy(P.ev(), dst.ap(), bk.ap(), [bk], [dst])
                qsl = qp.v3(NTs)[:, :, b * LS:(b + 1) * LS]
                P.copy("pool", qsl, qT.v3(NTs)[:, :, b * LS:(b + 1) * LS], [qT, qp], [qp])
                bs = P.bank2()
                for h in range(4):
                    for k2 in range(2):
                        kc = 2 * h + k2
                        P.mm(bs.ap(0, NTs, h * NMEM, (h + 1) * NMEM), qp.ap(c0=kc * NTs, c1=(kc + 1) * NTs),
                             kt.ap(c0=kc * NMEM, c1=(kc + 1) * NMEM), k2 == 0, k2 == 1, [qp, kt], [bs])
                for half in range(2):
                    sd = sc.sub(half * 512, 512)
                    P.tt("dve", sd.ap(0, NTs), sd.ap(0, NTs), bs.ap(0, NTs, half * 512, half * 512 + 512), ALU.add,
                         [sd, bs], [sd])
                P.ts("pool", qsl, qT.v3(NTs)[:, :, b * LS:(b + 1) * LS], 0.0, None, ALU.mult, None, [qT, qp], [qp])
            softmax_rows(sc, NTs, pe, small)
            pT = ba(8 * NTs)
            vbf = [ba(2 * D), ba(2 * D)]
            bt = P.bank()
            for h in range(4):
                for mt in range(2):
                    o_ = (h * 2 + mt) * NTs
                    P.tr(bt.ap(c0=o_, c1=o_ + NTs), pe.ap(0, NTs, h * NMEM + mt * 128, h * NMEM + mt * 128 + 128),
                         ident.ap(0, NTs, 0, NTs), [pe, ident], [bt])
            P.copy("act", pT.ap(), bt.ap(), [bt], [pT])
            bo = P.bank()
            P.hold(bo)
            for b in range(NSQ):
                vb_ = kbuf[b % 4]
                P.dma("sp", vb_.v3(D), cv[b].rearrange("(t p) d -> p t d", p=128), writes=[vb_])
                vh = vbf[b % 2]
                P.copy("act", vh.ap(c1=D), vb_.ap(c1=D), [vb_], [vh.sub(0, D)])
                P.copy("dve", vh.ap(c0=D), vb_.ap(c0=D), [vb_], [vh.sub(D, D)])
                for oc in range(8):
                    h = oc // 2
                    for mt in range(2):
                        o_ = (h * 2 + mt) * NTs + b * LS
                        P.mm(bo.ap(c0=oc * NTs + b * LS, c1=oc * NTs + (b + 1) * LS),
                             vh.ap(c0=mt * D + oc * 128, c1=mt * D + oc * 128 + 128), pT.ap(c0=o_, c1=o_ + LS),
                             mt == 0, mt == 1, [vh, pT], [bo])
            P.copy("act", yT.ap(), bo.ap(), [bo], [yT])
            P.unhold(bo)
        AFr.release(m0)
        ARr.release(mR)
        ABr.release(mBa)

    def group(kind, g, last, pre=False, nxt=None, h_ready=False, hoist=False):
        prompt = kind == "p"
        NT = NTP if prompt else NSQ * LS
        nseq, L = (1, NT) if prompt else (NSQ, LS)
        C = 128 if prompt else 64
        pt = min(128, NT)
        ntl = max(1, NT // 128)
        src_x = xp[g * NT:(g + 1) * NT, :] if prompt else xs
        dst_y = o_yp[g * NT:(g + 1) * NT, :] if prompt else o_ys
        mF = AFr.mark()
        mR = ARr.mark()
        jobs = Jobs()

        xs1 = F32View(Buf(ABr, 0, 2 * ntl * D))
        xs2 = F32View(Buf(ABr, 2 * ntl * D, 2 * ntl * D))
        hT = ra(KC * NT)
        U = ra(16 * NT)
        yT = U.sub(0, KC * NT)
        qT = U.sub(KC * NT, KC * NT)
        actT = U

        def phase0(_):
            m = AFr.mark()
            xT = fa(KC * NT)
            load_xT(src_x, NT, xT, stage=xs1 if pre else None)
            rmsnorm_fm(xT, vc("norm1_g"), NT, hT)
            AFr.release(m)
        if not h_ready:
            jobs.add(None, phase0)

        mRW = AFr.mark()
        ss = fa(NSC * NT)
        SW = nseq * (L + 1)
        szraw = [fa(SW), fa(SW)]
        dtmp = [fa(NT), fa(NT)]
        stT = sso = None
        if not prompt:
            stT = fa(NSC * NSQ)
            sso = fa(NSC * NSQ)

            def load_state(_):
                m = AFr.mark()
                stok = fa(PSH)
                P.dma("sp", stok.ap(0, NSQ), sshift, writes=[stok])
                for sc in range(NSC):
                    bk = P.bank()
                    P.tr(bk.ap(c1=NSQ), stok.ap(0, NSQ, sc * 128, sc * 128 + 128), ident.ap(0, NSQ, 0, NSQ),
                         [stok, ident], [bk])
                    P.copy(P.ev(), stT.ap(c0=sc * NSQ, c1=(sc + 1) * NSQ), bk.ap(c1=NSQ), [bk],
                           [stT.sub(sc * NSQ, NSQ)])
                AFr.release(m)
            jobs.add(None, load_state)
        cnt = {"i": 0}

        def shift_sink(sc, bk):
            i = cnt["i"]
            cnt["i"] += 1
            sr = szraw[i % 2]
            dt_ = dtmp[i % 2]
            sr3 = sr.v3(L + 1)
            P.copy("act", sr3[:, :, 1:L + 1], bk.ap(c1=NT).rearrange("p (s l) -> p s l", l=L), [bk], [sr])
            if prompt:
                cc = carry.ap(c0=sc, c1=sc + 1).rearrange("p (s l) -> p s l", l=1)
                P.copy("act", sr3[:, :, 0:1], cc, [carry, sr], [sr])
                P.copy("act", cc, sr3[:, :, L:L + 1], [sr, carry], [carry])
            else:
                P.copy("pool", sr3[:, :, 0:1], stT.ap(c0=sc * NSQ, c1=(sc + 1) * NSQ).rearrange("p (s l) -> p s l", l=1),
                       [stT, sr], [sr])
                P.copy("pool", sso.ap(c0=sc * NSQ, c1=(sc + 1) * NSQ).rearrange("p (s l) -> p s l", l=1),
                       sr3[:, :, L:L + 1], [sr], [sso.sub(sc * NSQ, NSQ)])
            d3 = dt_.v3(L)
            P.tt("dve" if prompt else "pool", d3, sr3[:, :, 0:L], sr3[:, :, 1:L + 1], ALU.subtract, [sr], [dt_])
            dst = ss.sub(sc * NT, NT)
            P.stt("dve", dst.v3(L), d3, mucols.ap(c0=sc, c1=sc + 1), sr3[:, :, 1:L + 1], ALU.mult, ALU.add,
                  [dt_, mucols, sr], [dst])

        for blk in range(6):
            jobs.add(wsrc("w_in", 0, 2048 + blk * 512, 512),
                     lambda wb, blk=blk: linear_fm(wb, 512, hT, NT, shift_sink, oc0=blk * 4))
        jobs.add(wsrc("w_in", 0, 5120, 256), lambda wb: linear_fm(wb, 256, hT, NT, shift_sink, oc0=24))

        def rwkv_job(_):
            P.dump("ss%d" % g, ss.ap(), ss, [128, NSC * NT])
            rwkv_group(prompt, NT, nseq, L, C, ss, yT, g, sso, last)
        jobs.add(None, rwkv_job)

        merged = [None]

        def alloc_merged(_):
            AFr.release(mRW)
            merged[0] = fa(KC * NT)
        jobs.add(None, alloc_merged)

        def branch_jobs(n, first):
            gbuf = {}
            mk = {}

            def begin(_):
                mk["m"] = AFr.mark()
                for oc in range(8):
                    gbuf[oc] = fa(NT)

            def gate_job(half):
                def f(wb):
                    def sink(oc, bk):
                        P.act(gbuf[oc].ap(), bk.ap(c1=NT), AF.Sigmoid, [bk], [gbuf[oc]])
                    linear_fm(wb, 512, hT, NT, sink, oc0=half * 4)
                return f

            def proj_job(half):
                def f(wb):
                    def sink(oc, bk):
                        md = merged[0].sub(oc * NT, NT)
                        gt = gbuf[oc]
                        if first:
                            P.tt("dve", md.ap(), bk.ap(c1=NT), gt.ap(), ALU.mult, [bk, gt], [md])
                        else:
                            P.tt("dve", gt.ap(), bk.ap(c1=NT), gt.ap(), ALU.mult, [bk, gt], [gt])
                            P.tt("pool", md.ap(), md.ap(), gt.ap(), ALU.add, [md, gt], [md])
                    linear_fm(wb, 512, yT, NT, sink, oc0=half * 4)
                return f

            def end(_):
                AFr.release(mk["m"])
            jobs.add(None, begin)
            for half in range(2):
                jobs.add(wsrc("w_in", 0, 6400 + n * 1024 + half * 512, 512), gate_job(half))
            for half in range(2):
                jobs.add(wsrc("w_branch%d" % n, 0, half * 512, 512), proj_job(half))
            jobs.add(None, end)

        branch_jobs(1, True)
        if prompt and g == 0:
            add_mem_jobs(jobs, qT)

        gm = {}

        def gmlp_begin(_):
            gm["m"] = AFr.mark()
            gm["uT"] = fa(KC * NT)
            gm["vn"] = fa(ntl * D)
            gm["lng"] = fa(D)
            gm["lnb"] = fa(D)
            P.dma("sp", gm["lng"].ap(), ln_v_g.partition_broadcast(128), writes=[gm["lng"]])
            P.dma("sp", gm["lnb"].ap(), ln_v_b.partition_broadcast(128), writes=[gm["lnb"]])
            gm["s1"] = fa(2 * ntl)
            gm["WgT"] = fa(8 * C)
            gm["sgb"] = fa(8 * C)
            WgT = gm["WgT"]
            mI = cbuf("mI4") if prompt else cbuf("mI4s")
            if prompt:
                raw = fa(8 * 128)
                gm["raw"] = raw
                P.dma("sp", raw.v3(128), sg_w.rearrange("g t s -> t g s"), writes=[raw])
                P.dma("sp", gm["sgb"].ap(), sg_b.rearrange("g t -> (g t)").partition_broadcast(128),
                      writes=[gm["sgb"]])

                def build_wgt():
                    for gi in range(8):
                        bk = P.bank()
                        P.tr(bk.ap(c1=128), raw.ap(c0=gi * 128, c1=gi * 128 + 128), ident.ap(), [raw, ident], [bk])
                        P.tt("dve", WgT.ap(c0=gi * 128, c1=gi * 128 + 128), bk.ap(c1=128), mI.ap(c1=128), ALU.mult,
                             [bk, mI], [WgT.sub(gi * 128, 128)])
            else:
                w4t = fa(8 * 4)
                for gi in range(8):
                    P.dma("sp", w4t.ap(0, 4, gi * 4, gi * 4 + 4), sg_w[gi, 0:4, 0:4].rearrange("t s -> s t"),
                          writes=[w4t], allow_slow_non_contiguous=True)
                esel = cbuf("esel")
                bsb = fa(64)
                sgbp = fa(8 * 128)
                P.dma("sp", sgbp.ap(), sg_b.rearrange("g t -> (g t)").partition_broadcast(128), writes=[sgbp])

                def build_wgt():
                    for gi in range(8):
                        bk = P.bank()
                        P.mm(bk.ap(0, 4, 0, 64), w4t.ap(0, 4, gi * 4, gi * 4 + 4), esel.ap(0, 4, 0, 64), True, True,
                             [w4t, esel], [bk])
                        P.copy("act", bsb.ap(0, 4), bk.ap(0, 4, 0, 64), [bk], [bsb])
                        bk2 = P.bank()
                        P.mm(bk2.ap(0, 64, 0, 64), bsb.ap(0, 4), esel.ap(0, 4, 0, 64), True, True, [bsb, esel], [bk2])
                        P.tt("dve", WgT.ap(0, 64, gi * 64, gi * 64 + 64), bk2.ap(0, 64, 0, 64), mI.ap(0, 64, 0, 64),
                             ALU.mult, [bk2, mI], [WgT.sub(gi * 64, 64)])
                    P.copy("pool", gm["sgb"].v4(NSQ, LS), sgbp.pat([[128, 8], [0, NSQ], [1, LS]]), [sgbp], [gm["sgb"]])
            gm["build_wgt"] = build_wgt
        jobs.add(None, gmlp_begin)

        def u_job(half):
            def f(wb):
                def sink(oc, bk):
                    dst = gm["uT"].sub(oc * NT, NT)
                    P.act(dst.ap(), bk.ap(c1=NT), AF.Gelu_apprx_tanh, [bk], [dst])
                linear_fm(wb, 512, hT, NT, sink, oc0=half * 4)
            return f

        def v_job(half):
            def f(wb):
                for ti in range(ntl):
                    bk = P.bank()
                    for kc in range(KC):
                        P.mm(bk.ap(0, pt), hT.ap(c0=kc * NT + ti * 128, c1=kc * NT + ti * 128 + pt),
                             wb.ap(c0=kc * 512, c1=(kc + 1) * 512), kc == 0, kc == KC - 1, [hT, wb], [bk])
                    dst = gm["vn"].sub(ti * D + half * 512, 512)
                    s1c = gm["s1"].sub(ti * 2 + half, 1)
                    P.act(dst.ap(0, pt), bk.ap(0, pt), AF.Gelu_apprx_tanh, [bk], [dst, s1c], accum_out=s1c.ap(0, pt))
            return f
        for half in range(2):
            jobs.add(wsrc("w_in", 0, half * 512, 512), u_job(half))
        for half in range(2):
            jobs.add(wsrc("w_in", 0, 1024 + half * 512, 512), v_job(half))

        def gmlp_core(_):
            gm["build_wgt"]()
            m2 = AFr.mark()
            junk = fa(D)
            st2 = fa(8)
            tmpb = [fa(512), fa(512)]
            for ti in range(ntl):
                v = gm["vn"].sub(ti * D, D)
                s1 = gm["s1"].sub(ti * 2, 2)
                nm = st2.sub(0, 1)
                P.tt("dve", nm.ap(0, pt), s1.ap(0, pt, 0, 1), s1.ap(0, pt, 1, 2), ALU.add, [s1], [nm])
                P.ts("dve", nm.ap(0, pt), nm.ap(0, pt), -1.0 / D, None, ALU.mult, None, [nm], [nm])
                P.ts("dve", v.ap(0, pt), v.ap(0, pt), nm.ap(0, pt), None, ALU.add, None, [v, nm], [v])
                s2 = st2.sub(1, 1)
                P.act(junk.ap(0, pt), v.ap(0, pt), AF.Square, [v], [junk, s2], accum_out=s2.ap(0, pt))
                rs = st2.sub(2, 1)
                P.act(rs.ap(0, pt), s2.ap(0, pt), AF.Sqrt, [s2, epsb], [rs], bias=eps_ap(1, 0, pt), scale=1.0 / D)
                P.op("dve", lambda e, rs=rs: e.reciprocal(out=rs.ap(0, pt), in_=rs.ap(0, pt)), [rs], [rs])
                P.stt("dve", v.ap(0, pt), v.ap(0, pt), rs.ap(0, pt), gm["lng"].ap(0, pt), ALU.mult, ALU.mult,
                      [v, rs, gm["lng"]], [v])
                P.tt("pool", v.ap(0, pt), v.ap(0, pt), gm["lnb"].ap(0, pt), ALU.add, [v, gm["lnb"]], [v])
                if not prompt:
                    P.dma("sp", o_sgv, v.ap(0, pt), reads=[v])
                P.dump("vn%d_%d" % (g, ti), v.ap(0, pt), v, [pt, D])
                per = 512 // C
                nb_ = (8 + per - 1) // per
                banks = [P.bank() for _ in range(nb_)]
                for gi in range(8):
                    bk = banks[gi // per]
                    co = (gi % per) * C
                    P.mm(bk.ap(c0=co, c1=co + C), v.ap(0, pt, gi * 128, gi * 128 + 128),
                         gm["WgT"].ap(0, pt, gi * C, gi * C + C), True, True, [v, gm["WgT"]], [bk])
                for bi, bk in enumerate(banks):
                    ng = min(per, 8 - bi * per)
                    tmp = tmpb[bi % 2].sub(0, ng * C)
                    P.tt("dve", tmp.ap(), bk.ap(c1=ng * C), gm["sgb"].ap(c0=bi * per * C, c1=(bi * per + ng) * C),
                         ALU.add, [bk, gm["sgb"]], [tmp])
                    u3 = gm["uT"].v3(NT)[:, bi * per:bi * per + ng, ti * 128:ti * 128 + C]
                    y3 = yT.v3(NT)[:, bi * per:bi * per + ng, ti * 128:ti * 128 + C]
                    ur = Reg("F", [(gm["uT"].lo + (bi * per + k) * NT + ti * 128, C) for k in range(ng)])
                    yr = Reg("R", [(yT.lo + (bi * per + k) * NT + ti * 128, C) for k in range(ng)])
                    P.tt("pool", y3, tmp.v3(C), u3, ALU.mult, [tmp, ur], [yr])
            AFr.release(m2)
        jobs.add(None, gmlp_core)
        jobs.add(None, lambda _: AFr.release(gm["m"]))
        branch_jobs(0, False)

        def q_job(half):
            def f(wb):
                if half == 0 and prompt:
                    stage_x(src_x, NT, xs2)

                def sink(oc, bk):
                    dst = qT.sub(oc * NT, NT)
                    P.copy(P.ev(), dst.ap(), bk.ap(c1=NT), [bk], [dst])
                linear_fm(wb, 512, hT, NT, sink, oc0=half * 4)
            return f
        for half in range(2):
            jobs.add(wsrc("w_in", 0, 5376 + half * 512, 512), q_job(half))
        jobs.add(None, lambda _: attention(prompt, NT, qT, yT, g))
        branch_jobs(2, False)

        fin = {}

        def fin_begin(_):
            for kc in range(KC):
                P.copy(P.ev(), yT.ap(c0=kc * NT, c1=(kc + 1) * NT), merged[0].ap(c0=kc * NT, c1=(kc + 1) * NT),
                       [merged[0].sub(kc * NT, NT)], [yT.sub(kc * NT, NT)])
            P.dump("merged%d" % g, merged[0].ap(), merged[0], [128, KC * NT])
            AFr.release(mRW)
            fin["xT"] = fa(KC * NT)
            load_xT(src_x, NT, fin["xT"], stage=xs2 if prompt else None)
            if nxt is not None:
                stage_x(nxt, NT, xs1)
        jobs.add(None, fin_begin)

        def out_job(half):
            def f(wb):
                def sink(oc, bk):
                    x1 = fin["xT"].sub(oc * NT, NT)
                    P.tt("dve", x1.ap(), x1.ap(), bk.ap(c1=NT), ALU.add, [x1, bk], [x1])
                linear_fm(wb, 512, yT, NT, sink, oc0=half * 4)
            return f
        for half in range(2):
            jobs.add(wsrc("w_out", 0, half * 512, 512), out_job(half))

        def norm2(_):
            P.dump("x1T%d" % g, fin["xT"].ap(), fin["xT"], [128, KC * NT])
            rmsnorm_fm(fin["xT"], vc("norm2_g"), NT, hT)
            fin["r"] = [fa(NT), fa(NT)]
        jobs.add(None, norm2)

        def up_job(blk):
            def f(wb):
                def sink(oc, bk):
                    r = fin["r"][oc % 2]
                    P.act(r.ap(), bk.ap(c1=NT), AF.Relu, [bk], [r])
                    dst = actT.sub((oc % 16) * NT, NT)
                    P.stt("dve", dst.ap(), bk.ap(c1=NT), 0.0, r.ap(), ALU.max, ALU.mult, [bk, r], [dst])
                linear_fm(wb, 512, hT, NT, sink, oc0=blk * 4)
            return f

        dbanks = {}

        def down_job(cb, kb):
            def f(wb):
                kbl = kb % 2
                if kbl == 0:
                    dbanks[cb] = [P.bank() for _ in range(4)]
                for oc in range(4):
                    bk = dbanks[cb][oc]
                    for kc in range(8):
                        a = actT.sub((kbl * 8 + kc) * NT, NT)
                        P.mm(bk.ap(c1=NT), wb.ap(c0=kc * 512 + oc * 128, c1=kc * 512 + oc * 128 + 128), a.ap(),
                             kbl == 0 and kc == 0, kbl == 1 and kc == 7, [wb, a], [bk])
                    if kbl == 1:
                        x2 = fin["xT"].sub((cb * 4 + oc) * NT, NT)
                        P.tt("dve", x2.ap(), x2.ap(), bk.ap(c1=NT), ALU.add, [x2, bk], [x2])
            return f
        for hf in range(2):
            for blk in range(4):
                jobs.add(wsrc("w_up", 0, (hf * 4 + blk) * 512, 512), up_job(hf * 4 + blk))
            for cb in range(2):
                for kbl in range(2):
                    kb = hf * 2 + kbl
                    jobs.add(wsrc("w_down", kb * 1024, cb * 512, 512), down_job(cb, kb))

        def final(_):
            m2 = AFr.mark()
            yfin = fa(KC * NT)
            rmsnorm_fm(fin["xT"], vc("norm_f_g"), NT, yfin)
            ytok = fa(ntl * D)
            for ti in range(ntl):
                for half in range(2):
                    bk = P.bank()
                    for q in range(4):
                        kc = half * 4 + q
                        P.tr(bk.ap(0, pt, q * 128, q * 128 + 128), yfin.ap(c0=kc * NT + ti * 128, c1=kc * NT + ti * 128 + pt),
                             ident.ap(), [yfin, ident], [bk])
                    dst = ytok.sub(ti * D + half * 512, 512)
                    P.copy(P.ev(), dst.ap(0, pt), bk.ap(0, pt), [bk], [dst])
            if NT >= 128:
                P.dma("sp", dst_y.rearrange("(t p) d -> p t d", p=128), ytok.v3(D), reads=[ytok])
            else:
                P.dma("sp", dst_y, ytok.ap(0, pt), reads=[ytok])
            AFr.release(m2)
        def next_phase0(_):
            m = AFr.mark()
            xT2 = fa(KC * NT)
            load_xT(nxt, NT, xT2, stage=xs1)
            rmsnorm_fm(xT2, vc("norm1_g"), NT, hT)
            AFr.release(m)
        if hoist and nxt is not None:
            jobs.add(None, next_phase0)
        jobs.add(None, final)

        jobs.run()
        AFr.release(mF)
        ARr.release(mR)

    for wname, ncolsT, krows in (("w_in", PIN, D), ("w_branch0", D, D), ("w_branch1", D, D), ("w_branch2", D, D),
                                ("w_out", D, D), ("w_up", DFF, D), ("w_down", D, DFF)):
        for r0 in range(0, krows, 1024):
            c0 = 0
            while c0 < ncolsT:
                nco = 512
                if wname == "w_in" and c0 == 5120:
                    nco = 256
                wsrc(wname, r0, c0, nco)
                c0 += nco
    assert len(wblocks) == NWB, len(wblocks)
    for g in range(ngroups):
        nx = xp[(g + 1) * NTP:(g + 2) * NTP, :] if g + 1 < ngroups else None
        group("p", g, g == ngroups - 1, pre=g > 0, nxt=nx, h_ready=g > 0, hoist=True)
    if do_sample:
        AFr.release(markF_sample)
        group("s", 0, True)

    P.S.finish()
    P.S.emit(nc)
    st.close()
    P.peakF = AFr.peak
    P.peakR = ARr.peak
    return P


_PROG = {}


def _core_inputs(c, inp):
    f = lambda a: np.ascontiguousarray(np.asarray(a, dtype=np.float32))
    m = {
        "xp": f(inp["x_prompt"][c]),
        "xs": f(inp["x_sample"][NSQ * c:NSQ * (c + 1)].reshape(NSQ * LS, D)),
        "memp": f(inp["mem_prompt"][c]),
        "ck": f(inp["cache_mem_k"][0, NSQ * c:NSQ * (c + 1)].reshape(NSQ, NMEM, D)),
        "cv": f(inp["cache_mem_v"][0, NSQ * c:NSQ * (c + 1)].reshape(NSQ, NMEM, D)),
        "sshift": f(inp["state_shift"][0, NSQ * c:NSQ * (c + 1)]),
        "swkv": f(inp["state_wkv"][0, NSQ * c:NSQ * (c + 1)]),
        "consts": CONSTS,
        "onesd": np.ones((128, 128), np.float32),
        "shift_mu": f(inp["shift_mu"][0]),
        "ln_v_g": f(inp["ln_v_g"][0]),
        "ln_v_b": f(inp["ln_v_b"][0]),
        "sg_w": f(inp["sg_w"][0]),
        "sg_b": f(inp["sg_b"][0]),
        "w_in": f(inp["w_in"][0]),
        "w_w2": f(inp["w_w2"][0]),
        "w_a2": f(inp["w_a2"][0]),
        "w_g2": f(inp["w_g2"][0]),
        "w_mem_k": f(inp["w_mem_k"][0]),
        "w_mem_v": f(inp["w_mem_v"][0]),
        "w_branch": f(inp["w_branch"][0]),
        "w_out": f(inp["w_out"][0]),
        "w_up": f(inp["w_up"][0]),
        "w_down": f(inp["w_down"][0]),
    }
    for n in VEC_NAMES:
        a = inp[n]
        a = a if n == "norm_f_g" else a[0]
        m[n] = f(a).reshape(D)
    return m


def kernel(**inputs):
    if "p" not in _PROG:
        _PROG["p"] = build_program()
    P = _PROG["p"]
    in_maps = [_core_inputs(c, inputs) for c in range(NCORES)]
    res = run_bass_kernel_spmd(P.nc, in_maps, core_ids=list(range(NCORES)))
    R = res.results
    g = lambda name: [np.asarray(R[c][name], dtype=np.float32) for c in range(NCORES)]
    y_prompt = np.stack(g("o_yp"), 0)
    y_sample = np.concatenate(g("o_ys"), 0).reshape(NCORES * NSQ, LS, D)
    prompt_shift = np.stack(g("o_pshift"), 0)[None]
    prompt_wkv = np.stack(g("o_pwkv"), 0)[None]
    prompt_mem_k = np.stack(g("o_pmk"), 0).reshape(1, NCORES, NMEM, 4, 256)
    prompt_mem_v = np.stack(g("o_pmv"), 0).reshape(1, NCORES, NMEM, 4, 256)
    sample_shift = np.concatenate(g("o_sshift"), 0)[None]
    sample_wkv = np.concatenate(g("o_swkv"), 0)[None]
    sample_gmlp_v = np.concatenate(g("o_sgv"), 0).reshape(1, NCORES * NSQ, LS, D)
    return (y_prompt, y_sample, prompt_shift, prompt_wkv, prompt_mem_k, prompt_mem_v,
            sample_shift, sample_wkv, sample_gmlp_v)
```

```python
import contextlib
import numpy as np
import concourse.bass as bass
import concourse.mybir as mybir
from concourse.bass_utils import run_bass_kernel_spmd

F32 = mybir.dt.float32
F32R = mybir.dt.float32r
BF16 = mybir.dt.bfloat16
AF = mybir.ActivationFunctionType
ALU = mybir.AluOpType
AX = mybir.AxisListType

NCORES = 8
D = 1024
KC = 8
SEQ = 2048
NSQ = 16
LS = 4
NMEM = 256
PSH = 3328
NSC = 26
PIN = 9472
DFF = 4096
NTP = 256
NGP = SEQ // NTP
RMS_EPS = 1e-6
LN_EPS = 1e-5
GN_EPS = 64e-5
EXPM05 = float(np.exp(-0.5))

ENGS = ("pe", "act", "dve", "pool", "sp")
N_DMA_SLOTS = 36
GR = 32

DEBUG = {}


class Reg:
    __slots__ = ("sp", "ranges")

    def __init__(self, sp, ranges):
        self.sp = sp
        self.ranges = ranges

    def keys(self):
        base = {"F": 0, "R": 100000, "P": 200000, "B": 300000, "D": 400000}[self.sp]
        out = []
        for lo, n in self.ranges:
            out.extend(range(base + lo // GR, base + (lo + n - 1) // GR + 1))
        return out


class Sched:
    def __init__(self):
        self.streams = {e: [] for e in ENGS}
        self.count = {e: 0 for e in ENGS}
        self.clock = {e: {} for e in ENGS}
        self.tok_clock = {}
        self.last_write = {}
        self.readers = {}
        self.n_dma = 0
        self.dma_slot_val = [0] * N_DMA_SLOTS
        self.n_waits = 0
        self.sw_uses = {}
        self.n_once = 0
        self.sw_tick = {}

    def _deps(self, rkeys, wkeys):
        need = {}
        lw = self.last_write
        rd = self.readers
        for k in rkeys:
            t = lw.get(k)
            if t is not None and need.get(t[0], 0) < t[1]:
                need[t[0]] = t[1]
        for k in wkeys:
            t = lw.get(k)
            if t is not None and need.get(t[0], 0) < t[1]:
                need[t[0]] = t[1]
            r = rd.get(k)
            if r:
                for s, v in r.items():
                    if need.get(s, 0) < v:
                        need[s] = v
        for s in list(need):
            tk = self.sw_tick.get(s)
            if tk is not None and need.get(tk[0], 0) < tk[1]:
                need[tk[0]] = tk[1]
        return need

    def _emit_waits(self, eng, need):
        clk = self.clock[eng]
        for s, v in need.items():
            if eng == "pe" and s == "pe":
                continue
            if clk.get(s, 0) >= v:
                continue
            self.streams[eng].append(("wait", s, v))
            self.n_waits += 1
            clk[s] = v
            snap = self.tok_clock.get((s, v))
            if snap:
                for s2, v2 in snap.items():
                    if clk.get(s2, 0) < v2:
                        clk[s2] = v2

    def _record(self, tok, rkeys, wkeys):
        lw = self.last_write
        rd = self.readers
        ws = set(wkeys)
        for k in wkeys:
            lw[k] = tok
            rd[k] = None
        s, v = tok
        for k in rkeys:
            if k in ws:
                continue
            r = rd.get(k)
            if r is None:
                rd[k] = {s: v}
            elif r.get(s, 0) < v:
                r[s] = v

    @staticmethod
    def _keys(regs):
        out = []
        for r in regs:
            out.extend(r.keys())
        return out

    def op(self, eng, fn, reads=(), writes=()):
        rk = self._keys(reads)
        wk = self._keys(writes)
        self._emit_waits(eng, self._deps(rk, wk))
        self.count[eng] += 1
        tok = (eng, self.count[eng])
        self.streams[eng].append(("op", fn, eng))
        self.tok_clock[tok] = dict(self.clock[eng])
        self._record(tok, rk, wk)

    def dma_sw(self, q, fn, ring, tick, reads=(), writes=()):
        rk = self._keys(reads)
        wk = self._keys(writes)
        need = self._deps(rk, wk)
        self._emit_waits(q, need)
        self.sw_uses[ring] = self.sw_uses.get(ring, 0) + 1
        skey = ("w", ring, self.sw_uses[ring])
        if self.sw_uses[ring] > 1:
            self.streams[q].append(("clear", ("w", ring)))
            self.count[q] += 1
            self.streams[q].append(("op", tick, q))
            self.tok_clock[(q, self.count[q])] = dict(self.clock[q])
            self.sw_tick[skey] = (q, self.count[q])
        tok = (skey, 16)
        self.streams[q].append(("dma", fn, skey))
        self.tok_clock[tok] = dict(self.clock[q])
        self._record(tok, rk, wk)

    def dma_once(self, q, fn, reads=(), writes=()):
        rk = self._keys(reads)
        wk = self._keys(writes)
        self._emit_waits(q, self._deps(rk, wk))
        self.n_once += 1
        skey = ("o", self.n_once)
        tok = (skey, 16)
        self.streams[q].append(("dma", fn, skey))
        self.tok_clock[tok] = dict(self.clock[q])
        self._record(tok, rk, wk)

    def dma(self, q, fn, reads=(), writes=()):
        rk = self._keys(reads)
        wk = self._keys(writes)
        need = self._deps(rk, wk)
        slot = self.n_dma % N_DMA_SLOTS
        self.n_dma += 1
        prev = self.dma_slot_val[slot]
        skey = ("d", slot)
        if prev > 0 and need.get(skey, 0) < prev:
            need[skey] = prev
        self._emit_waits(q, need)
        val = prev + 16
        self.dma_slot_val[slot] = val
        tok = (skey, val)
        self.streams[q].append(("dma", fn, skey))
        self.tok_clock[tok] = dict(self.clock[q])
        self._record(tok, rk, wk)

    def finish(self, q="sp"):
        need = {}
        for slot in range(N_DMA_SLOTS):
            if self.dma_slot_val[slot] > 0:
                need[("d", slot)] = self.dma_slot_val[slot]
        for e in ("pe", "act", "dve", "pool"):
            if self.count[e] > 0:
                need[e] = self.count[e]
        for i in range(1, self.n_once + 1):
            need[("o", i)] = 16
        self._emit_waits(q, need)

    def emit(self, nc):
        with contextlib.ExitStack() as st:
            sems = {}
            for e in ("pe", "act", "dve", "pool"):
                sems[e] = st.enter_context(nc.semaphore("s_" + e))
            for i in range(N_DMA_SLOTS):
                sems[("d", i)] = st.enter_context(nc.semaphore("s_d%d" % i))
            for i in range(1, self.n_once + 1):
                sems[("o", i)] = st.enter_context(nc.semaphore("s_o%d" % i))
            block = st.enter_context(nc.Block())

            def hw(key):
                return sems[key[:2]] if key[0] == "w" else sems[key]

            def run(engobj, items):
                pend = []
                for it in items:
                    if it[0] == "wait":
                        pend.append((hw(it[1]), it[2]))
                    elif it[0] == "op":
                        for s_, v_ in pend[:-1]:
                            engobj.wait_ge(s_, v_)
                        ins = it[1](engobj)
                        if pend:
                            ins._wait_ge(pend[-1][0], pend[-1][1])
                        ins.then_inc(sems[it[2]], 1)
                        pend = []
                    elif it[0] == "clear":
                        engobj.sem_clear(sems[it[1]])
                    else:
                        for s_, v_ in pend:
                            engobj.wait_ge(s_, v_)
                        pend = []
                        it[1](engobj).then_inc(hw(it[2]), 16)
                for s_, v_ in pend:
                    engobj.wait_ge(s_, v_)

            @block.tensor
            def _(e):
                run(e, self.streams["pe"])

            @block.scalar
            def _(e):
                run(e, self.streams["act"])

            @block.vector
            def _(e):
                run(e, self.streams["dve"])

            @block.gpsimd
            def _(e):
                run(e, self.streams["pool"])

            @block.sync
            def _(e):
                run(e, self.streams["sp"])


class Arena:
    def __init__(self, sp, tensor_ap, size):
        self.sp = sp
        self.A = tensor_ap
        self.tensor = tensor_ap.tensor
        self.size = size
        self.top = 0
        self.peak = 0

    def alloc(self, n):
        lo = self.top
        self.top += n
        self.peak = max(self.peak, self.top)
        if self.top > self.size:
            raise RuntimeError("arena %s overflow: %d > %d" % (self.sp, self.top, self.size))
        return Buf(self, lo, n)

    def mark(self):
        return self.top

    def release(self, m):
        self.top = m


class Buf:
    def __init__(self, arena, lo, n):
        self.ar = arena
        self.lo = lo
        self.n = n
        self.reg = Reg(arena.sp, [(lo, n)])

    def sub(self, off, n):
        assert off + n <= self.n, (off, n, self.n)
        return Buf(self.ar, self.lo + off, n)

    def ap(self, p0=0, p1=128, c0=0, c1=None):
        c1 = self.n if c1 is None else c1
        return self.ar.A[p0:p1, self.lo + c0:self.lo + c1]

    def v3(self, b, p0=0, p1=128):
        return self.ap(p0, p1).rearrange("p (a b) -> p a b", b=b)

    def v4(self, b, c, p0=0, p1=128):
        return self.ap(p0, p1).rearrange("p (a b c) -> p a b c", b=b, c=c)

    def pat(self, dims, off=0, p0=0, p1=128):
        base = self.ar.A[p0:p1, self.lo + off:self.lo + off + 1]
        return bass.AP(self.ar.tensor, base.offset, [[self.ar.size, p1 - p0]] + [list(d) for d in dims])

    def rsub(self, ranges):
        return Reg(self.ar.sp, [(self.lo + o, n) for o, n in ranges])


class F32View:
    def __init__(self, buf):
        self.buf = buf
        self.reg = buf.reg
        self.full = buf.ap().bitcast(F32)

    def ap(self, p0=0, p1=128, c0=0, c1=None):
        c1 = self.full.shape[1] if c1 is None else c1
        return self.full[p0:p1, c0:c1]

    def v3(self, b, p0=0, p1=128):
        return self.full[p0:p1, :].rearrange("p (a b) -> p a b", b=b)

    def sub(self, off, n):
        return self


def cols_reg(buf, rowlen, c0, C, nrows):
    return buf.rsub([(r * rowlen + c0, C) for r in range(nrows)])


class Prog:
    def __init__(self, dbg=None):
        self.dbg = dbg or {}
        self.nc = bass.Bass("TRN2", target_bir_lowering=False)
        self.S = Sched()
        self.ins = {}
        self.outs = {}
        self.bank_rr = 0
        self.alt = 0
        self.held = set()

    def din(self, name, shape):
        t = self.nc.dram_tensor(name, list(shape), F32, kind="ExternalInput").ap()
        self.ins[name] = t
        return t

    def dout(self, name, shape):
        t = self.nc.dram_tensor(name, list(shape), F32, kind="ExternalOutput").ap()
        self.outs[name] = t
        return t

    def bank(self):
        b = self.bank_rr
        while b in self.held:
            b = (b + 1) % 8
        self.bank_rr = (b + 1) % 8
        return Buf(self.PS, b * 512, 512)

    def hold(self, buf):
        for b in range(buf.lo // 512, (buf.lo + buf.n - 1) // 512 + 1):
            self.held.add(b)

    def unhold(self, buf):
        for b in range(buf.lo // 512, (buf.lo + buf.n - 1) // 512 + 1):
            self.held.discard(b)

    def bank2(self):
        b = self.bank_rr
        if b % 2:
            b = (b + 1) % 8
        while b in self.held or (b + 1) in self.held:
            b = (b + 2) % 8
        self.bank_rr = (b + 2) % 8
        return Buf(self.PS, b * 512, 1024)

    def banks_for(self, n):
        return self.bank() if n <= 512 else self.bank2()

    def ev(self):
        self.alt ^= 1
        return "act" if self.alt else "dve"

    def op(self, eng, fn, reads=(), writes=()):
        self.S.op(eng, fn, [getattr(r, 'reg', r) for r in reads],
                  [getattr(w, 'reg', w) for w in writes])

    def dma(self, q, out, in_, reads=(), writes=(), ring=None, **kw):
        rr = [getattr(r, 'reg', r) for r in reads]
        ww = [getattr(w, 'reg', w) for w in writes]
        if q == "pool":
            self.S.dma_once(q, lambda e: e.dma_start(out=out, in_=in_, **kw), rr, ww)
        else:
            self.S.dma(q, lambda e: e.dma_start(out=out, in_=in_, **kw), rr, ww)

    def mm(self, out, lhsT, rhs, start, stop, reads, writes):
        self.op("pe", lambda e: e.matmul(out, lhsT=lhsT, rhs=rhs, start=start, stop=stop), reads, writes)

    def tr(self, out, in_, ident, reads, writes):
        self.op("pe", lambda e: e.transpose(out=out, in_=in_, identity=ident), reads, writes)

    def act(self, out, in_, func, reads, writes, **kw):
        self.op("act", lambda e: e.activation(out=out, in_=in_, func=func, **kw), reads, writes)

    def copy(self, eng, out, in_, reads, writes):
        if eng == "act":
            self.op("act", lambda e: e.activation(out=out, in_=in_, func=AF.Copy), reads, writes)
        else:
            self.op(eng, lambda e: e.tensor_copy(out=out, in_=in_), reads, writes)

    def tt(self, eng, out, in0, in1, op, reads, writes):
        self.op(eng, lambda e: e.tensor_tensor(out=out, in0=in0, in1=in1, op=op), reads, writes)

    def ts(self, eng, out, in0, s1, s2, op0, op1, reads, writes):
        if s2 is None:
            self.op(eng, lambda e: e.tensor_scalar(out=out, in0=in0, scalar1=s1, scalar2=None, op0=op0), reads, writes)
        else:
            self.op(eng, lambda e: e.tensor_scalar(out=out, in0=in0, scalar1=s1, scalar2=s2, op0=op0, op1=op1),
                    reads, writes)

    def stt(self, eng, out, in0, scalar, in1, op0, op1, reads, writes):
        eng = "dve"
        self.op(eng, lambda e: e.scalar_tensor_tensor(out=out, in0=in0, scalar=scalar, in1=in1, op0=op0, op1=op1),
                reads, writes)

    def dump(self, name, ap, buf, shape):
        if not self.dbg.get(name):
            return
        t = self.dout("dbg_" + name, shape)
        self.dma("sp", t, ap, reads=[buf])


def make_consts():
    c = {}
    idx = np.arange(128)
    c["ident"] = np.eye(128, dtype=np.float32)
    blk = (idx[:, None] // 64 == idx[None, :] // 64).astype(np.float32)
    c["blk64"] = blk
    ms = (idx[:, None] < idx[None, :]).astype(np.float32)
    mi = (idx[:, None] <= idx[None, :]).astype(np.float32)
    mst = (idx[:, None] > idx[None, :]).astype(np.float32)
    c["mS4"] = np.tile(ms, (1, 4))
    c["mI4"] = np.tile(mi, (1, 4))
    c["mST4"] = np.tile(mst, (1, 4))
    seg = np.ones((128, 8, 128), np.float32)
    seg[:, :, 0] = 0.0
    c["segP"] = seg.reshape(128, 1024)
    i64 = np.arange(64)
    same = (i64[:, None] // LS == i64[None, :] // LS)
    z = np.zeros((128, 256), np.float32)
    a = z.copy(); a[:64] = np.tile((same & (i64[:, None] < i64[None, :])).astype(np.float32), (1, 4)); c["mS4s"] = a
    a = z.copy(); a[:64] = np.tile((same & (i64[:, None] <= i64[None, :])).astype(np.float32), (1, 4)); c["mI4s"] = a
    a = z.copy(); a[:64] = np.tile((same & (i64[:, None] > i64[None, :])).astype(np.float32), (1, 4)); c["mST4s"] = a
    seg = np.ones((128, 8, NSQ, LS), np.float32)
    seg[:, :, :, 0] = 0.0
    c["segS"] = seg.reshape(128, 512)
    e = np.zeros((128, 64), np.float32)
    for t in range(64):
        e[t % LS, t] = 1.0
    c["esel"] = e
    rm = np.zeros((128, NSQ), np.float32)
    for t in range(64):
        rm[t, t // LS] = 1.0
    c["rowmask"] = rm
    names = ["ident", "blk64", "mS4", "mI4", "mST4", "segP", "mS4s", "mI4s", "mST4s", "segS", "esel", "rowmask"]
    offs = {}
    o = 0
    for n in names:
        offs[n] = (o, c[n].shape[1])
        o += c[n].shape[1]
    arr = np.concatenate([c[n] for n in names], axis=1).astype(np.float32)
    return arr, offs


CONSTS, COFF = make_consts()
NCONST = CONSTS.shape[1]

VEC_NAMES = ["norm1_g", "norm2_g", "norm_f_g", "mem_norm_g", "w0", "a0", "k_k", "k_a", "r_k", "lnx_g", "lnx_b"]


def build_program(dbg=None, ngroups=NGP, do_sample=True):
    P = Prog(dbg)
    nc = P.nc
    xp = P.din("xp", [SEQ, D])
    xs = P.din("xs", [NSQ * LS, D])
    memp = P.din("memp", [NMEM, D])
    ck = P.din("ck", [NSQ, NMEM, D])
    cv = P.din("cv", [NSQ, NMEM, D])
    sshift = P.din("sshift", [NSQ, PSH])
    swkv = P.din("swkv", [NSQ, 16, 64, 64])
    consts = P.din("consts", [128, NCONST])
    onesd = P.din("onesd", [128, 128])
    vec = {n: P.din(n, [D]) for n in VEC_NAMES}
    shift_mu = P.din("shift_mu", [PSH])
    ln_v_g = P.din("ln_v_g", [D])
    ln_v_b = P.din("ln_v_b", [D])
    sg_w = P.din("sg_w", [8, 128, 128])
    sg_b = P.din("sg_b", [8, 128])
    w_in = P.din("w_in", [D, PIN])
    w_w2 = P.din("w_w2", [64, D])
    w_a2 = P.din("w_a2", [64, D])
    w_g2 = P.din("w_g2", [128, D])
    w_mem_k = P.din("w_mem_k", [D, D])
    w_mem_v = P.din("w_mem_v", [D, D])
    w_branch = P.din("w_branch", [3, D, D])
    w_out = P.din("w_out", [D, D])
    w_up = P.din("w_up", [D, DFF])
    w_down = P.din("w_down", [DFF, D])

    o_yp = P.dout("o_yp", [SEQ, D])
    o_ys = P.dout("o_ys", [NSQ * LS, D])
    o_pshift = P.dout("o_pshift", [PSH])
    o_pwkv = P.dout("o_pwkv", [16, 64, 64])
    o_pmk = P.dout("o_pmk", [NMEM, D])
    o_pmv = P.dout("o_pmv", [NMEM, D])
    o_sshift = P.dout("o_sshift", [NSQ, PSH])
    o_swkv = P.dout("o_swkv", [NSQ, 16, 64, 64])
    o_sgv = P.dout("o_sgv", [NSQ * LS, D])

    st = contextlib.ExitStack()
    RSIZE = 33500
    BSIZE = 20600
    FSIZE = 53000 - RSIZE // 2 - BSIZE // 2
    arF_t = st.enter_context(nc.sbuf_tensor("arenaF", [128, FSIZE], F32))
    arR_t = st.enter_context(nc.sbuf_tensor("arenaR", [128, RSIZE], BF16))
    NWB = 43
    wscr = nc.dram_tensor("wscr", [NWB, 128, 4096], BF16).ap()
    arB_t = st.enter_context(nc.sbuf_tensor("arenaB", [128, BSIZE], BF16))
    ABr = Arena("B", arB_t[:, :], BSIZE)
    ba = ABr.alloc
    ps_t = st.enter_context(nc.psum_tensor("psum", [128, 4096], F32))
    AFr = Arena("F", arF_t[:, :], FSIZE)
    ARr = Arena("R", arR_t[:, :], RSIZE)
    P.PS = Arena("P", ps_t[:, :], 4096)
    fa = AFr.alloc
    ra = ARr.alloc

    cst = fa(NCONST)
    P.dma("sp", cst.ap(), consts, writes=[cst])

    def cbuf(name):
        o, n = COFF[name]
        return cst.sub(o, n)

    ident = cbuf("ident")
    blk64 = cbuf("blk64")
    onesR = ra(128)
    epsb = fa(8)
    for i, v in enumerate([RMS_EPS, LN_EPS, GN_EPS, 1e-12, 0.0]):
        P.op("dve", lambda e, i=i, v=v: e.memset(epsb.ap(c0=i, c1=i + 1), v), writes=[epsb])

    P.tick = lambda e: e.memset(epsb.ap(c0=7, c1=8), 0.0)
    P.op("pool", lambda e: e.memset(onesR.ap(), 1.0), writes=[onesR])

    def eps_ap(i, p0=0, p1=128):
        return epsb.ap(p0, p1, i, i + 1)

    vrows = fa(128)
    P.op("pool", lambda e: e.memset(vrows.ap(), 0.0), writes=[vrows])
    for i, n in enumerate(VEC_NAMES):
        P.dma("sp", vrows.ap(8 * i, 8 * i + 8), vec[n].rearrange("(k p) -> k p", p=128), writes=[vrows])
    murows = fa(128)
    P.op("pool", lambda e: e.memset(murows.ap(), 0.0), writes=[murows])
    P.dma("sp", murows.ap(0, NSC), shift_mu.rearrange("(k p) -> k p", p=128), writes=[murows])
    vcols = fa(128)
    mucols = fa(32)
    bk = P.bank()
    P.tr(bk.ap(c0=0, c1=128), vrows.ap(), ident.ap(), [vrows, ident], [bk])
    P.tr(bk.ap(c0=128, c1=256), murows.ap(), ident.ap(), [murows, ident], [bk])
    P.copy("dve", vcols.ap(), bk.ap(c0=0, c1=128), [bk], [vcols])
    P.copy("dve", mucols.ap(), bk.ap(c0=128, c1=160), [bk], [mucols])

    def vc(name):
        i = VEC_NAMES.index(name)
        return vcols.sub(8 * i, 8)

    omka = fa(8)
    P.ts("dve", omka.ap(), vc("k_a").ap(), -1.0, 1.0, ALU.mult, ALU.add, [vcols], [omka])
    lw12 = fa(1024)
    P.dma("sp", lw12.ap(0, 64), w_w2, writes=[lw12])
    P.dma("sp", lw12.ap(64, 128), w_a2, writes=[lw12])
    lwg = fa(1024)
    P.dma("sp", lwg.ap(), w_g2, writes=[lwg])
    markF_sample = AFr.mark()
    carry = fa(NSC)
    P.op("dve", lambda e: e.memset(carry.ap(), 0.0), writes=[carry])
    STp = fa(512)
    P.op("dve", lambda e: e.memset(STp.ap(), 0.0), writes=[STp])
    Vm = fa(2 * D)
    KTm = ra(KC * NMEM)
    NRING = 6
    wbufs = [ra(4096) for _ in range(NRING)]

    wstate = {"i": 0}
    wblocks = {}
    wconverted = set()

    def wfetch(key, kcn, ncols):
        ring = wstate["i"] % NRING
        b = wbufs[ring]
        wstate["i"] += 1
        wb = b.sub(0, kcn * ncols)
        if isinstance(key, tuple) and key[0] == "direct":
            P.dma("pool", wb.v3(ncols), key[1], writes=[b])
        else:
            bi = wblocks[key]
            dreg = Reg("D", [(bi * GR, 1)])
            if key not in wconverted:
                wconverted.add(key)
                wname, r0, c0, nco = key
                P.dma("pool", wb.v3(ncols), src3(wtensors[wname], r0, c0, nco, kcn), writes=[b])
                P.dma("sp", wscr[bi][:, 0:kcn * ncols], wb.ap(), reads=[b], writes=[dreg])
            else:
                P.dma("sp", wb.ap(), wscr[bi][:, 0:kcn * ncols], reads=[dreg], writes=[b])
        return wb

    wtensors = {"w_in": w_in, "w_out": w_out, "w_up": w_up, "w_down": w_down,
                "w_branch0": w_branch[0], "w_branch1": w_branch[1], "w_branch2": w_branch[2]}

    def src3(w2d, r0, c0, ncols, kcn=8):
        return w2d[r0:r0 + 128 * kcn, c0:c0 + ncols].rearrange("(kc p) n -> p kc n", p=128)

    def wsrc(wname, r0, c0, ncols, kcn=8):
        key = (wname, r0, c0, ncols)
        if key not in wblocks:
            wblocks[key] = len(wblocks)
        return (key, kcn, ncols)

    def wsrc_direct(w2d, r0, c0, ncols, kcn=8):
        return (("direct", src3(w2d, r0, c0, ncols, kcn)), kcn, ncols)

    def convert_weights():
        for key, bi in sorted(wblocks.items(), key=lambda kv: kv[1]):
            wname, r0, c0, ncols = key
            ring = wstate["i"] % NRING
            b = wbufs[ring]
            wstate["i"] += 1
            wb = b.sub(0, 8 * ncols)
            P.dma("pool", wb.v3(ncols), src3(wtensors[wname], r0, c0, ncols), writes=[b])
            P.dma("sp", wscr[bi][:, 0:8 * ncols], wb.ap(), reads=[b], writes=[Reg("D", [(bi * GR, 1)])])

    class Jobs:
        def __init__(self):
            self.jobs = []

        def add(self, src, fn):
            self.jobs.append((src, fn))

        def run(self):
            jobs = self.jobs
            n = len(jobs)
            fetched = {}
            wl_ = [i for i in range(n) if jobs[i][0] is not None]
            for i in range(n):
                if jobs[i][0] is not None and i not in fetched:
                    fetched[i] = wfetch(*jobs[i][0])
                for j in wl_:
                    if j > i and j not in fetched and len(fetched) < NRING:
                        fetched[j] = wfetch(*jobs[j][0])
                jobs[i][1](fetched.pop(i, None))

    def linear_fm(wb, ncols, actT, NT, sink, oc0=0, kcn=8):
        for oc in range(ncols // 128):
            bk = P.bank()
            for kc in range(kcn):
                P.mm(bk.ap(c1=NT), wb.ap(c0=kc * ncols + oc * 128, c1=kc * ncols + oc * 128 + 128),
                     actT.ap(c0=kc * NT, c1=(kc + 1) * NT), kc == 0, kc == kcn - 1,
                     [wb, actT.sub(kc * NT, NT)], [bk])
            sink(oc0 + oc, bk)

    def stage_x(src2d, NT, xtok):
        if NT >= 128:
            P.dma("sp", xtok.v3(D), src2d.rearrange("(t p) d -> p t d", p=128), writes=[xtok])
        else:
            P.dma("sp", xtok.ap(0, min(128, NT)), src2d, writes=[xtok])

    def load_xT(src2d, NT, xT, stage=None):
        m = AFr.mark()
        ntl = max(1, NT // 128)
        pt = min(128, NT)
        if stage is None:
            xtok = fa(ntl * D)
            stage_x(src2d, NT, xtok)
        else:
            xtok = stage
        for kc in range(KC):
            bk = P.bank()
            for ti in range(ntl):
                P.tr(bk.ap(c0=ti * 128, c1=ti * 128 + pt), xtok.ap(0, pt, ti * D + kc * 128, ti * D + kc * 128 + 128),
                     ident.ap(0, pt, 0, pt), [xtok, ident], [bk])
            P.copy(P.ev(), xT.ap(c0=kc * NT, c1=(kc + 1) * NT), bk.ap(c1=NT), [bk], [xT.sub(kc * NT, NT)])
        AFr.release(m)

    def rmsnorm_fm(xT, gcol, NT, hT):
        mF = AFr.mark()
        mR = ARr.mark()
        sq = [ra(NT), ra(NT)]
        ssb = P.bank()
        for kc in range(KC):
            s = sq[kc % 2]
            P.act(s.ap(), xT.ap(c0=kc * NT, c1=(kc + 1) * NT), AF.Square, [xT.sub(kc * NT, NT)], [s])
            P.mm(ssb.ap(c1=NT), onesR.ap(), s.ap(), kc == 0, kc == KC - 1, [onesR, s], [ssb])
        rstd = fa(NT)
        P.act(rstd.ap(), ssb.ap(c1=NT), AF.Sqrt, [ssb, epsb], [rstd], bias=eps_ap(0), scale=1.0 / D)
        P.op("dve", lambda e: e.reciprocal(out=rstd.ap(), in_=rstd.ap()), [rstd], [rstd])
        for kc in range(KC):
            P.stt("dve" if kc % 2 else "pool", hT.ap(c0=kc * NT, c1=(kc + 1) * NT), xT.ap(c0=kc * NT, c1=(kc + 1) * NT),
                  gcol.ap(c0=kc, c1=kc + 1), rstd.ap(), ALU.mult, ALU.mult,
                  [xT.sub(kc * NT, NT), gcol, rstd], [hT.sub(kc * NT, NT)])
        AFr.release(mF)
        ARr.release(mR)

    def prologue_mem():
        mF = AFr.mark()
        mR = ARr.mark()
        mT = fa(KC * NMEM)
        load_xT(memp, NMEM, mT)
        hmT = ra(KC * NMEM)
        rmsnorm_fm(mT, vc("mem_norm_g"), NMEM, hmT)
        stage = fa(2 * D)
        jobs = Jobs()

        def kjob(half):
            def f(wb):
                def sink(oc, bk):
                    P.copy(P.ev(), KTm.ap(c0=oc * NMEM, c1=(oc + 1) * NMEM), bk.ap(c1=NMEM), [bk],
                           [KTm.sub(oc * NMEM, NMEM)])
                linear_fm(wb, 512, hmT, NMEM, sink, oc0=half * 4)
                for mt in range(2):
                    bk = P.bank()
                    for kc in range(KC):
                        P.mm(bk.ap(), hmT.ap(c0=kc * NMEM + mt * 128, c1=kc * NMEM + mt * 128 + 128),
                             wb.ap(c0=kc * 512, c1=(kc + 1) * 512), kc == 0, kc == KC - 1, [hmT, wb], [bk])
                    dst = stage.sub(mt * D + half * 512, 512)
                    P.copy(P.ev(), dst.ap(), bk.ap(), [bk], [dst])
            return f

        def vjob(half):
            def f(wb):
                for mt in range(2):
                    bk = P.bank()
                    for kc in range(KC):
                        P.mm(bk.ap(), hmT.ap(c0=kc * NMEM + mt * 128, c1=kc * NMEM + mt * 128 + 128),
                             wb.ap(c0=kc * 512, c1=(kc + 1) * 512), kc == 0, kc == KC - 1, [hmT, wb], [bk])
                    dst = Vm.sub(mt * D + half * 512, 512)
                    P.copy(P.ev(), dst.ap(), bk.ap(), [bk], [dst])
            return f

        for half in range(2):
            jobs.add(wsrc_direct(w_mem_k, 0, half * 512, 512), kjob(half))
        jobs.add(None, lambda wb: P.dma("sp", o_pmk.rearrange("(t p) d -> p t d", p=128), stage.v3(D), reads=[stage]))
        for half in range(2):
            jobs.add(wsrc_direct(w_mem_v, 0, half * 512, 512), vjob(half))
        jobs.add(None, lambda wb: P.dma("sp", o_pmv.rearrange("(t p) d -> p t d", p=128), Vm.v3(D), reads=[Vm]))
        jobs.run()
        AFr.release(mF)
        ARr.release(mR)

    prologue_mem()

    def rwkv_group(prompt, NT, nseq, L, C, ss, yT, g, sso, last):
        m0 = AFr.mark()
        mB0 = ABr.mark()
        CW = 8 * C
        nch = NT // C
        nlev = 7 if prompt else 2
        mS = cbuf("mS4") if prompt else cbuf("mS4s")
        mI = cbuf("mI4") if prompt else cbuf("mI4s")
        mST = cbuf("mST4") if prompt else cbuf("mST4s")
        segm = cbuf("segP") if prompt else cbuf("segS")
        rowmask = cbuf("rowmask")
        W4 = 4 * C
        XW = 4 * 64

        def bc8(colbuf):
            return colbuf.pat([[1, 8], [0, C]])

        if prompt:
            STs = None
            STb = ba(512)
            P.copy("pool", STb.ap(), STp.ap(), [STp], [STb])
        else:
            STs = fa(nseq * 512)
            s0 = [fa(1024), fa(1024)]
            for b in range(nseq):
                sb_ = s0[b % 2]
                P.dma("sp", sb_.ap(0, 64).rearrange("p (h j) -> p h j", j=64), swkv[b].rearrange("h i j -> i h j"),
                      writes=[sb_])
                bk = P.bank()
                for kc in range(8):
                    P.tr(bk.ap(c0=kc * 64, c1=kc * 64 + 64), sb_.ap(0, 64, kc * 128, kc * 128 + 128),
                         ident.ap(0, 64, 0, 64), [sb_, ident], [bk])
                dst = STs.sub(b * 512, 512)
                P.copy(P.ev(), dst.ap(), bk.ap(), [bk], [dst])
            STb = ba(nseq * 128)
            Apad = [ba(nseq * C), ba(nseq * C)]
            Rpad = [ba(nseq * C), ba(nseq * C)]
            for pb in Apad + Rpad:
                P.op("pool", lambda e, pb=pb: e.memset(pb.ap(), 0.0), [], [pb])
            tmpBs = [ba(1024), ba(1024)]
            tmpKs = [ba(1024), ba(1024)]

        TW = max(CW, 1024)
        t0, t1, t2, t3 = fa(CW), fa(CW), fa(CW), fa(CW)
        t4f, t5f = fa(TW), fa(TW)
        t4, t5 = t4f.sub(0, CW), t5f.sub(0, CW)
        t10, t11 = fa(CW), fa(TW)
        t11c = t11.sub(0, CW)
        Bt, Kt = ba(CW), ba(CW)
        AtP = [ba(CW), ba(CW)]
        RtP = [ba(CW), ba(CW)]
        for pb in AtP + RtP:
            P.op("pool", lambda e, pb=pb: e.memset(pb.ap(), 0.0), [], [pb])
        Vtok, BhT, KhT, Utok = ba(1024), ba(1024), ba(1024), ba(1024)
        Ytok = t4f
        lor = fa(2 * C)
        wc = fa(8 * nseq)
        gst = fa(64)
        nsets = 2 if prompt else 1
        MB = [dict(Pm=[ba(W4), ba(W4)], PTm=[ba(W4), ba(W4)], Tm=[ba(W4), ba(W4)], mak=ba(W4), mbr=ba(W4), mkr=ba(W4),
                   xts=ba(XW)) for _ in range(nsets)]
        hb_groups = [(0, 1), (2, 3)] if prompt else [(0,), (1,), (2,), (3,)]
        ss3 = ss.v3(NT)

        for ci in range(nch):
            c0 = ci * C

            def sl(sc0, p0=0, p1=128):
                return ss.v3(NT, p0, p1)[:, sc0:sc0 + 8, c0:c0 + C]

            def slr(sc0, n=8):
                return cols_reg(ss.sub(sc0 * NT, n * NT), NT, c0, C, n)
            r3, k3, v3 = sl(0), sl(8), sl(16)
            rR, kR, vR = slr(0), slr(8), slr(16)
            loraR = slr(24, 2)
            wl = ss.v3(NT, 0, 64)[:, 24, c0:c0 + C]
            al = ss.v3(NT, 64, 128)[:, 24, c0:c0 + C]
            gl = ss3[:, 25, c0:c0 + C]

            def f3(buf, p0=0, p1=128):
                return buf.v3(C, p0, p1)

            P.act(lor.ap(0, 64, 0, C), wl, AF.Tanh, [loraR], [lor])
            P.act(lor.ap(0, 128, C, 2 * C), gl, AF.Sigmoid, [loraR], [lor])
            P.tt("pool", f3(t5), k3, bc8(vc("k_k")), ALU.mult, [kR, vcols], [t5])
            bkt = P.bank2()
            for kc in range(8):
                P.tr(bkt.ap(0, C, kc * 128, kc * 128 + 128), v3[:, kc, :], ident.ap(), [vR, ident], [bkt])
            for half in range(2):
                P.copy("dve" if half else "act", Vtok.ap(0, C, half * 512, half * 512 + 512),
                       bkt.ap(0, C, half * 512, half * 512 + 512), [bkt], [Vtok.sub(half * 512, 512)])
            bka = P.banks_for(CW)
            for kc in range(8):
                P.mm(bka.ap(c0=kc * C, c1=(kc + 1) * C), lw12.ap(64, 128, kc * 128, kc * 128 + 128), al,
                     True, True, [lw12, loraR], [bka])
            bkw = P.banks_for(CW)
            for kc in range(8):
                P.mm(bkw.ap(c0=kc * C, c1=(kc + 1) * C), lw12.ap(0, 64, kc * 128, kc * 128 + 128), lor.ap(0, 64, 0, C),
                     True, True, [lw12, lor], [bkw])
            P.act(t4.ap(), t5.ap(), AF.Square, [t5], [t4])
            for kc in range(8):
                P.act(t0.ap(c0=kc * C, c1=(kc + 1) * C), bkw.ap(c0=kc * C, c1=(kc + 1) * C), AF.Sigmoid,
                      [bkw, vcols], [t0.sub(kc * C, C)], bias=vc("w0").ap(c0=kc, c1=kc + 1), scale=1.0)
            bkk = P.banks_for(CW)
            for kc in range(8):
                P.mm(bkk.ap(c0=kc * C, c1=(kc + 1) * C), blk64.ap(), t4.ap(c0=kc * C, c1=(kc + 1) * C), True, True,
                     [blk64, t4.sub(kc * C, C)], [bkk])
            P.ts("dve", t0.ap(), t0.ap(), -EXPM05, None, ALU.mult, None, [t0], [t0])
            for kc in range(8):
                P.act(t10.ap(c0=kc * C, c1=(kc + 1) * C), bka.ap(c0=kc * C, c1=(kc + 1) * C), AF.Sigmoid,
                      [bka, vcols], [t10.sub(kc * C, C)], bias=vc("a0").ap(c0=kc, c1=kc + 1), scale=1.0)
            P.op("dve", lambda e: e.tensor_tensor_scan(out=t1.ap(), data0=segm.ap(c1=CW), data1=t0.ap(),
                                                       initial=0.0, op0=ALU.mult, op1=ALU.add),
                 [segm, t0], [t1])
            P.act(t4.ap(), bkk.ap(c1=CW), AF.Sqrt, [bkk, epsb], [t4], bias=eps_ap(3), scale=1.0)
            P.tt("dve", t0.ap(), t1.ap(), t0.ap(), ALU.subtract, [t0, t1], [t0])
            P.op("dve", lambda e: e.reciprocal(out=t4.ap(), in_=t4.ap()), [t4], [t4])
            P.act(t2.ap(), t1.ap(), AF.Exp, [t1], [t2])
            P.tt("dve", t5.ap(), t5.ap(), t4.ap(), ALU.mult, [t5, t4], [t5])
            P.act(t3.ap(), t1.ap(), AF.Exp, [t1], [t3], scale=-1.0)
            P.act(t0.ap(), t0.ap(), AF.Exp, [t0], [t0])
            for kc in range(8):
                P.act(t4.ap(c0=kc * C, c1=(kc + 1) * C), t10.ap(c0=kc * C, c1=(kc + 1) * C), AF.Identity,
                      [t10.sub(kc * C, C), vcols, omka], [t4.sub(kc * C, C)], bias=omka.ap(c0=kc, c1=kc + 1),
                      scale=vc("k_a").ap(c0=kc, c1=kc + 1))
            P.tt("dve", f3(t4), f3(t4), k3, ALU.mult, [t4, kR], [t4])
            if prompt:
                P.copy("pool", wc.ap(), t2.pat([[C, 8]], off=C - 1), [t2], [wc])
            else:
                P.copy("pool", wc.v3(nseq), t2.pat([[C, 8], [L, nseq]], off=L - 1), [t2], [wc])
            for par in range(2):
                p0, p1 = par * 64, par * 64 + 64
                P.stt("dve", AtP[par].ap(p0, p1), t5.ap(p0, p1), -1.0, t0.ap(p0, p1), ALU.mult, ALU.mult, [t5, t0], [AtP[par]])
                P.tt("pool", f3(RtP[par], p0, p1), sl(0, p0, p1), f3(t2, p0, p1), ALU.mult, [rR, t2], [RtP[par]])
            P.tt("dve", t0.ap(), t5.ap(), t10.ap(), ALU.mult, [t5, t10], [t0])
            for kc in range(8):
                P.act(t1.ap(c0=kc * C, c1=(kc + 1) * C), r3[:, kc, :], AF.Copy, [rR, vcols], [t1.sub(kc * C, C)],
                      scale=vc("r_k").ap(c0=kc, c1=kc + 1))
            P.tt("dve", t1.ap(), t1.ap(), t4.ap(), ALU.mult, [t1, t4], [t1])
            bkb = P.banks_for(CW)
            for kc in range(8):
                P.mm(bkb.ap(c0=kc * C, c1=(kc + 1) * C), blk64.ap(), t1.ap(c0=kc * C, c1=(kc + 1) * C), True, True,
                     [blk64, t1.sub(kc * C, C)], [bkb])
            P.tt("dve", t11c.ap(), t0.ap(), t3.ap(), ALU.mult, [t0, t3], [t11c])
            P.tt("pool", t5.ap(), t4.ap(), t3.ap(), ALU.mult, [t4, t3], [t5])
            P.copy("act", Bt.ap(), t11c.ap(), [t11c], [Bt])
            P.copy("act", Kt.ap(), t5.ap(), [t5], [Kt])
            P.tt("dve", f3(t1), bkb.v3(C)[:, 0:8, :], v3, ALU.mult, [bkb, vR], [t1])
            if prompt:
                wcb = t2.pat([[C, 8], [0, C]], off=C - 1)
                P.tt("dve", f3(t11c), f3(t11c), wcb, ALU.mult, [t11c, t2], [t11c])
                P.tt("pool", f3(t5), f3(t5), wcb, ALU.mult, [t5, t2], [t5])
            else:
                wcb = t2.pat([[C, 8], [L, nseq], [0, L]], off=L - 1)
                P.tt("dve", t11c.v4(nseq, L), t11c.v4(nseq, L), wcb, ALU.mult, [t11c, t2], [t11c])
                P.tt("pool", t5.v4(nseq, L), t5.v4(nseq, L), wcb, ALU.mult, [t5, t2], [t5])
            def hq(hb, q):
                return 2 * hb + q // 2, (q % 2) * 64

            for hbs in hb_groups:
                cur = {}
                for hb in hbs:
                    S_ = MB[hb % len(MB)]
                    bA, bAT, bK, bBR, bKR = P.bank(), P.bank(), P.bank(), P.bank(), P.bank()
                    for q in range(4):
                        kc, pr = hq(hb, q)
                        At, Rt = AtP[q % 2], RtP[q % 2]
                        Bq = Bt.ap(c0=kc * C, c1=(kc + 1) * C)
                        Kq = Kt.ap(c0=kc * C, c1=(kc + 1) * C)
                        Aq = At.ap(c0=kc * C, c1=(kc + 1) * C)
                        Rq = Rt.ap(c0=kc * C, c1=(kc + 1) * C)
                        o = lambda bk_: bk_.ap(0, C, q * C, (q + 1) * C)
                        P.mm(o(bA), Bq, Aq, True, True, [Bt, At], [bA])
                        P.mm(o(bAT), Aq, Bq, True, True, [Bt, At], [bAT])
                        P.mm(o(bK), Kq, Aq, True, True, [Kt, At], [bK])
                        P.mm(o(bBR), Bq, Rq, True, True, [Bt, Rt], [bBR])
                        P.mm(o(bKR), Kq, Rq, True, True, [Kt, Rt], [bKR])
                    P.tt("dve", S_["Pm"][0].ap(0, C), bA.ap(0, C, 0, W4), mS.ap(0, C, 0, W4), ALU.mult, [bA, mS], [S_["Pm"][0]])
                    P.tt("dve", S_["PTm"][0].ap(0, C), bAT.ap(0, C, 0, W4), mST.ap(0, C, 0, W4), ALU.mult, [bAT, mST],
                         [S_["PTm"][0]])
                    P.tt("dve", S_["mak"].ap(0, C), bK.ap(0, C, 0, W4), mS.ap(0, C, 0, W4), ALU.mult, [bK, mS], [S_["mak"]])
                    P.tt("dve", S_["mbr"].ap(0, C), bBR.ap(0, C, 0, W4), mI.ap(0, C, 0, W4), ALU.mult, [bBR, mI], [S_["mbr"]])
                    P.tt("dve", S_["mkr"].ap(0, C), bKR.ap(0, C, 0, W4), mI.ap(0, C, 0, W4), ALU.mult, [bKR, mI], [S_["mkr"]])
                    P.tt("pool", S_["Tm"][0].v3(C, 0, C), S_["Pm"][0].v3(C, 0, C), ident.pat([[0, 4], [1, C]], p0=0, p1=C),
                         ALU.add, [S_["Pm"][0], ident], [S_["Tm"][0]])
                    cur[hb] = 0
                for lev in range(1, nlev):
                    lastl = lev == nlev - 1
                    bks = {}
                    for hb in hbs:
                        S_ = MB[hb % len(MB)]
                        c_ = cur[hb]
                        bP = None if lastl else P.bank()
                        bPT = P.bank()
                        bks[hb] = (bP, bPT)
                        for q in range(4):
                            Pq = S_["Pm"][c_].ap(0, C, q * C, (q + 1) * C)
                            PTq = S_["PTm"][c_].ap(0, C, q * C, (q + 1) * C)
                            P.mm(bPT.ap(0, C, q * C, (q + 1) * C), Pq, PTq, True, True, [S_["Pm"][c_], S_["PTm"][c_]], [bPT])
                        for q in range(4):
                            Pq = S_["Pm"][c_].ap(0, C, q * C, (q + 1) * C)
                            PTq = S_["PTm"][c_].ap(0, C, q * C, (q + 1) * C)
                            if not lastl:
                                P.mm(bP.ap(0, C, q * C, (q + 1) * C), PTq, Pq, True, True, [S_["Pm"][c_], S_["PTm"][c_]], [bP])
                    for hb in hbs:
                        S_ = MB[hb % len(MB)]
                        nx = cur[hb] ^ 1
                        bP, bPT = bks[hb]
                        P.copy("act", S_["PTm"][nx].ap(0, C), bPT.ap(0, C, 0, W4), [bPT], [S_["PTm"][nx]])
                        if not lastl:
                            P.copy("dve", S_["Pm"][nx].ap(0, C), bP.ap(0, C, 0, W4), [bP], [S_["Pm"][nx]])
                    bts = {}
                    for hb in hbs:
                        S_ = MB[hb % len(MB)]
                        c_ = cur[hb]
                        nx = c_ ^ 1
                        bT = P.bank()
                        bts[hb] = bT
                        for q in range(4):
                            P.mm(bT.ap(0, C, q * C, (q + 1) * C), S_["PTm"][nx].ap(0, C, q * C, (q + 1) * C),
                                 S_["Tm"][c_].ap(0, C, q * C, (q + 1) * C), True, True, [S_["PTm"][nx], S_["Tm"][c_]], [bT])
                    for hb in hbs:
                        S_ = MB[hb % len(MB)]
                        c_ = cur[hb]
                        nx = c_ ^ 1
                        P.tt("dve", S_["Tm"][nx].ap(0, C), S_["Tm"][c_].ap(0, C), bts[hb].ap(0, C, 0, W4), ALU.add,
                             [S_["Tm"][c_], bts[hb]], [S_["Tm"][nx]])
                        cur[hb] = nx
                bxs = {}
                for hb in hbs:
                    S_ = MB[hb % len(MB)]
                    mak_ = S_["mak"]
                    if not prompt:
                        P.copy("pool", STb.v4(2, 64), STs.v4(8, 64)[:, :, 2 * hb:2 * hb + 2, :], [STs], [STb])
                    bX = P.bank()
                    bX2 = None if prompt else P.bank()
                    bxs[hb] = (bX, bX2)
                    for q in range(4):
                        kc, pr = hq(hb, q)
                        h = 4 * hb + q
                        ox = bX.ap(0, C, q * 64, q * 64 + 64)
                        At = AtP[q % 2]
                        if prompt:
                            P.mm(ox, At.ap(c0=kc * C, c1=(kc + 1) * C), STb.ap(c0=kc * 64, c1=kc * 64 + 64),
                                 True, False, [At, STb], [bX])
                            P.mm(ox, mak_.ap(0, C, q * C, (q + 1) * C), Vtok.ap(0, C, h * 64, h * 64 + 64), False, True,
                                 [mak_, Vtok], [bX])
                        else:
                            ap_ = Apad[q % 2]
                            P.copy("pool", ap_.pat([[C + L, nseq], [1, L]], p0=pr, p1=pr + 64),
                                   At.ap(pr, pr + 64, kc * C, (kc + 1) * C).rearrange("p (b l) -> p b l", l=L), [At, ap_], [ap_])
                            for b in range(nseq):
                                o_ = b * 128 + (q // 2) * 64
                                P.mm(ox, ap_.ap(c0=b * C, c1=(b + 1) * C), STb.ap(c0=o_, c1=o_ + 64),
                                     b == 0, b == nseq - 1, [ap_, STb], [bX])
                            P.mm(bX2.ap(0, C, q * 64, q * 64 + 64), mak_.ap(0, C, q * C, (q + 1) * C),
                                 Vtok.ap(0, C, h * 64, h * 64 + 64), True, True, [mak_, Vtok], [bX2])
                for hb in hbs:
                    S_ = MB[hb % len(MB)]
                    bX, bX2 = bxs[hb]
                    xts = S_["xts"]
                    if prompt:
                        P.copy("act", xts.ap(0, C), bX.ap(0, C, 0, XW), [bX], [xts])
                    else:
                        ys = Ytok.sub(hb * XW, XW)
                        P.copy("act", ys.ap(0, C), bX.ap(0, C, 0, XW), [bX], [ys])
                        P.tt("dve", xts.ap(0, C), ys.ap(0, C), bX2.ap(0, C, 0, XW), ALU.add, [ys, bX2], [xts])
                bus = {}
                for hb in hbs:
                    S_ = MB[hb % len(MB)]
                    Tf = S_["Tm"][cur[hb]]
                    xts = S_["xts"]
                    bU = P.bank()
                    bus[hb] = bU
                    for q in range(4):
                        P.mm(bU.ap(0, C, q * 64, q * 64 + 64), Tf.ap(0, C, q * C, (q + 1) * C), xts.ap(0, C, q * 64, q * 64 + 64),
                             True, True, [Tf, xts], [bU])
                for hb in hbs:
                    ud = Utok.sub(hb * XW, XW)
                    P.copy("act", ud.ap(0, C), bus[hb].ap(0, C, 0, XW), [bus[hb]], [ud])
                bys = {}
                for hb in hbs:
                    S_ = MB[hb % len(MB)]
                    mbr_, mkr_ = S_["mbr"], S_["mkr"]
                    ud = Utok.sub(hb * XW, XW)
                    bY = P.bank()
                    bY2 = None if prompt else P.bank()
                    bys[hb] = (bY, bY2)
                    for q in range(4):
                        kc, pr = hq(hb, q)
                        h = 4 * hb + q
                        oy = bY.ap(0, C, q * 64, q * 64 + 64)
                        Rt = RtP[q % 2]
                        if prompt:
                            P.mm(oy, Rt.ap(c0=kc * C, c1=(kc + 1) * C), STb.ap(c0=kc * 64, c1=kc * 64 + 64),
                                 True, False, [Rt, STb], [bY])
                            oy2, by2, st2 = oy, bY, False
                        else:
                            rp_ = Rpad[q % 2]
                            P.copy("pool", rp_.pat([[C + L, nseq], [1, L]], p0=pr, p1=pr + 64),
                                   Rt.ap(pr, pr + 64, kc * C, (kc + 1) * C).rearrange("p (b l) -> p b l", l=L), [Rt, rp_], [rp_])
                            for b in range(nseq):
                                o_ = b * 128 + (q // 2) * 64
                                P.mm(oy, rp_.ap(c0=b * C, c1=(b + 1) * C), STb.ap(c0=o_, c1=o_ + 64),
                                     b == 0, b == nseq - 1, [rp_, STb], [bY])
                            oy2, by2, st2 = bY2.ap(0, C, q * 64, q * 64 + 64), bY2, True
                        P.mm(oy2, mbr_.ap(0, C, q * C, (q + 1) * C), Utok.ap(0, C, h * 64, h * 64 + 64), st2, False,
                             [mbr_, ud], [by2])
                        P.mm(oy2, mkr_.ap(0, C, q * C, (q + 1) * C), Vtok.ap(0, C, h * 64, h * 64 + 64), False, True,
                             [mkr_, Vtok], [by2])
                for hb in hbs:
                    bY, bY2 = bys[hb]
                    yd = Ytok.sub(hb * XW, XW)
                    P.copy("dve", yd.ap(0, C), bY.ap(0, C, 0, XW), [bY], [yd])
                    if not prompt:
                        P.tt("dve", yd.ap(0, C), yd.ap(0, C), bY2.ap(0, C, 0, XW), ALU.add, [yd, bY2], [yd])

            for srcb, src3, srcR, dst in ((t11c, None, t11c, BhT), (t5, None, t5, KhT)):
                bkt = P.bank2()
                for kc in range(8):
                    inp = src3[:, kc, :] if srcb is None else srcb.ap(c0=kc * C, c1=(kc + 1) * C)
                    P.tr(bkt.ap(0, C, kc * 128, kc * 128 + 128), inp, ident.ap(), [srcR, ident], [bkt])
                for half in range(2):
                    P.copy(P.ev(), dst.ap(0, C, half * 512, half * 512 + 512), bkt.ap(0, C, half * 512, half * 512 + 512),
                           [bkt], [dst.sub(half * 512, 512)])

            if not prompt:
                P.tt("dve", STs.v4(8, 64), STs.v4(8, 64), wc.pat([[1, nseq], [nseq, 8], [0, 64]]), ALU.mult, [STs, wc], [STs])
            def masks(b):
                P.act(tmpBs[b % 2].ap(0, C), BhT.ap(0, C), AF.Copy, [BhT, rowmask], [tmpBs[b % 2]],
                      scale=rowmask.ap(0, C, b, b + 1))
                P.act(tmpKs[b % 2].ap(0, C), KhT.ap(0, C), AF.Copy, [KhT, rowmask], [tmpKs[b % 2]],
                      scale=rowmask.ap(0, C, b, b + 1))
            if not prompt:
                masks(0)
            for b in range(nseq):
                if prompt:
                    lB, lK, STm = BhT, KhT, STp
                else:
                    lB, lK, STm = tmpBs[b % 2], tmpKs[b % 2], STs.sub(b * 512, 512)
                bS = P.bank2()
                for kc in range(8):
                    osl = bS.ap(c0=kc * 128, c1=kc * 128 + 128)
                    P.mm(osl, lB.ap(0, C, kc * 128, kc * 128 + 128), Utok.ap(0, C, kc * 128, kc * 128 + 128), True, False,
                         [lB, Utok], [bS])
                    P.mm(osl, lK.ap(0, C, kc * 128, kc * 128 + 128), Vtok.ap(0, C, kc * 128, kc * 128 + 128), False, True,
                         [lK, Vtok], [bS])
                for half in range(2):
                    p0, p1 = half * 64, half * 64 + 64
                    s3 = STm.v3(64, p0, p1)
                    if prompt:
                        wbc = wc.pat([[1, 8], [0, 64]], p0=p0, p1=p1)
                        P.tt("dve", s3, s3, wbc, ALU.mult, [STm, wc], [STm])
                    P.tt("dve", s3, s3, bS.v3(128, p0, p1)[:, :, half * 64:half * 64 + 64], ALU.add, [STm, bS], [STm])
                if prompt:
                    P.copy("pool", STb.ap(), STp.ap(), [STp], [STb])
                elif b + 1 < nseq:
                    masks(b + 1)
                if (prompt and last and ci == nch - 1) or (not prompt):
                    bo = P.bank2()
                    for kc in range(8):
                        P.tr(bo.ap(0, 64, kc * 128, kc * 128 + 128), STm.ap(c0=kc * 64, c1=kc * 64 + 64), ident.ap(),
                             [STm, ident], [bo])
                    so_full = t11 if prompt else [t5f, t11][b % 2]
                    P.copy("act", so_full.ap(0, 64, 0, 1024), bo.ap(0, 64, 0, 1024), [bo], [so_full])
                    dsto = o_pwkv if prompt else o_swkv[b]
                    P.dma("sp", dsto.rearrange("h i j -> i h j"), so_full.ap(0, 64, 0, 1024).rearrange("p (h j) -> p h j", j=64),
                          reads=[so_full])

            y16 = Ytok.v3(64, 0, C)
            s1 = gst.sub(0, 16)
            P.op("dve", lambda e: e.tensor_reduce(out=s1.ap(0, C), in_=y16, axis=AX.X, op=ALU.add), [Ytok], [s1])
            P.ts("dve", s1.ap(0, C), s1.ap(0, C), -1.0 / 64, None, ALU.mult, None, [s1], [s1])
            P.tt("dve", y16, y16, s1.pat([[1, 16], [0, 64]], p0=0, p1=C), ALU.add, [Ytok, s1], [Ytok])
            junk = t5f
            P.act(junk.ap(0, C, 0, 1024), Ytok.ap(0, C, 0, 1024), AF.Square, [Ytok], [junk])
            s2 = gst.sub(16, 16)
            P.op("dve", lambda e: e.tensor_reduce(out=s2.ap(0, C), in_=junk.v3(64, 0, C)[:, 0:16, :], axis=AX.X, op=ALU.add),
                 [junk], [s2])
            P.act(s2.ap(0, C), s2.ap(0, C), AF.Sqrt, [s2, epsb], [s2], bias=eps_ap(2, 0, C), scale=1.0 / 64)
            P.op("dve", lambda e: e.reciprocal(out=s2.ap(0, C), in_=s2.ap(0, C)), [s2], [s2])
            P.tt("dve", y16, y16, s2.pat([[1, 16], [0, 64]], p0=0, p1=C), ALU.mult, [Ytok, s2], [Ytok])
            bkn = P.banks_for(CW)
            for kc in range(8):
                P.tr(bkn.ap(0, 128, kc * C, (kc + 1) * C), Ytok.ap(0, C, kc * 128, kc * 128 + 128), ident.ap(0, C, 0, C),
                     [Ytok, ident], [bkn])
            y1 = t0
            for kc in range(8):
                P.act(y1.ap(c0=kc * C, c1=(kc + 1) * C), bkn.ap(c0=kc * C, c1=(kc + 1) * C), AF.Identity,
                      [bkn, vcols], [y1.sub(kc * C, C)], bias=vc("lnx_b").ap(c0=kc, c1=kc + 1),
                      scale=vc("lnx_g").ap(c0=kc, c1=kc + 1))
            P.tt("pool", y1.ap(), y1.ap(), t1.ap(), ALU.add, [y1, t1], [y1])
            bkg = P.banks_for(CW)
            for kc in range(8):
                P.mm(bkg.ap(c0=kc * C, c1=(kc + 1) * C), lwg.ap(c0=kc * 128, c1=kc * 128 + 128), lor.ap(c0=C, c1=2 * C),
                     True, True, [lwg, lor], [bkg])
            yo3 = yT.v3(NT)[:, :, c0:c0 + C]
            yoR = Reg("R", [(yT.lo + k * NT + c0, C) for k in range(8)])
            P.tt("dve", yo3, f3(y1), bkg.v3(C)[:, 0:8, :], ALU.mult, [y1, bkg], [yoR])

        if prompt and last:
            bk = P.bank()
            P.tr(bk.ap(0, NSC, 0, 128), carry.ap(), ident.ap(), [carry, ident], [bk])
            so = t4f
            P.copy("act", so.ap(0, NSC, 0, 128), bk.ap(0, NSC, 0, 128), [bk], [so])
            P.dma("sp", o_pshift.rearrange("(k p) -> k p", p=128), so.ap(0, NSC, 0, 128), reads=[so])
        if not prompt:
            so = Buf(AFr, t0.lo, PSH)
            for sc in range(NSC):
                bk = P.bank()
                P.tr(bk.ap(0, nseq, 0, 128), sso.ap(c0=sc * nseq, c1=(sc + 1) * nseq), ident.ap(), [sso, ident], [bk])
                P.copy(P.ev(), so.ap(0, nseq, sc * 128, sc * 128 + 128), bk.ap(0, nseq, 0, 128), [bk], [so])
            P.dma("sp", o_sshift, so.ap(0, nseq), reads=[so])
        AFr.release(m0)
        ABr.release(mB0)

    def softmax_rows(bs, pt, pe, small):
        mx = small.sub(0, 4)
        P.op("dve", lambda e: e.tensor_reduce(out=mx.ap(0, pt), in_=bs.v3(NMEM, 0, pt), axis=AX.X, op=ALU.max),
             [bs], [mx])
        P.ts("dve", mx.ap(0, pt), mx.ap(0, pt), -1.0 / 16, None, ALU.mult, None, [mx], [mx])
        rs = small.sub(4, 4)
        for h in range(4):
            P.act(pe.ap(0, pt, h * NMEM, (h + 1) * NMEM), bs.ap(0, pt, h * NMEM, (h + 1) * NMEM), AF.Exp,
                  [bs, mx], [pe.sub(h * NMEM, NMEM), rs], bias=mx.ap(0, pt, h, h + 1), scale=1.0 / 16,
                  accum_out=rs.ap(0, pt, h, h + 1))
        P.op("dve", lambda e: e.reciprocal(out=rs.ap(0, pt), in_=rs.ap(0, pt)), [rs], [rs])
        P.tt("dve", pe.v3(NMEM, 0, pt), pe.v3(NMEM, 0, pt), rs.pat([[1, 4], [0, NMEM]], p0=0, p1=pt), ALU.mult,
             [pe, rs], [pe])

    def attention(prompt, NT, qT, yT, g):
        m0 = AFr.mark()
        mR = ARr.mark()
        mBa = ABr.mark()
        small = fa(8)
        pe = fa(4 * NMEM)
        if prompt:
            nt_ = NT // 128
            pT = fa(1024)
            small2 = fa(8)
            pes = [pe, F32View(Buf(ABr, 8192, 2048))]
            pTs = [pT, F32View(Buf(ABr, 10240, 2048))]
            smalls = [small, small2]
            bss = []
            for ti in range(nt_):
                cs = ti * 128
                bs = P.bank2()
                bss.append(bs)
                for h in range(4):
                    for k2 in range(2):
                        kc = 2 * h + k2
                        P.mm(bs.ap(c0=h * NMEM, c1=(h + 1) * NMEM), qT.ap(c0=kc * NT + cs, c1=kc * NT + cs + 128),
                             KTm.ap(c0=kc * NMEM, c1=(kc + 1) * NMEM), k2 == 0, k2 == 1, [qT, KTm], [bs])
            for ti in range(nt_):
                softmax_rows(bss[ti], 128, pes[ti % 2], smalls[ti % 2])
            for ti in range(nt_):
                pe_, pT_ = pes[ti % 2], pTs[ti % 2]
                bt = P.bank2()
                for h in range(4):
                    for mt in range(2):
                        o_ = (h * 2 + mt) * 128
                        P.tr(bt.ap(c0=o_, c1=o_ + 128), pe_.ap(c0=h * NMEM + mt * 128, c1=h * NMEM + mt * 128 + 128),
                             ident.ap(), [pe_, ident], [bt])
                for half in range(2):
                    P.copy(P.ev(), pT_.ap(c0=half * 512, c1=half * 512 + 512), bt.ap(c0=half * 512, c1=half * 512 + 512),
                           [bt], [pT_.sub(half * 512, 512)])
            for ti in range(nt_):
                cs = ti * 128
                pT_ = pTs[ti % 2]
                bo = P.bank2()
                for oc in range(8):
                    h = oc // 2
                    for mt in range(2):
                        o_ = (h * 2 + mt) * 128
                        P.mm(bo.ap(c0=oc * 128, c1=oc * 128 + 128), Vm.ap(c0=mt * D + oc * 128, c1=mt * D + oc * 128 + 128),
                             pT_.ap(c0=o_, c1=o_ + 128), mt == 0, mt == 1, [Vm, pT_], [bo])
                yR = Reg("R", [(yT.lo + k * NT + cs, 128) for k in range(8)])
                P.copy("act", yT.v3(NT)[:, :, cs:cs + 128], bo.v3(128), [bo], [yR])
        else:
            NTs = NSQ * LS
            qp = ra(KC * NTs)
            P.ts("pool", qp.ap(), qT.ap(), 0.0, None, ALU.mult, None, [qT], [qp])
            kbuf = [fa(2 * D) for _ in range(4)]
            KTb = [ra(KC * NMEM), ra(KC * NMEM)]
            sc = fa(4 * NMEM)
            P.op("pool", lambda e: e.memset(sc.ap(), 0.0), [], [sc])
            for b in range(NSQ):
                kb_ = kbuf[b % 4]
                kt = KTb[b % 2]
                P.dma("sp", kb_.v3(D), ck[b].rearrange("(t p) d -> p t d", p=128), writes=[kb_])
                for kc2 in range(4):
                    bk = P.bank()
                    for k2 in range(2):
                        kc = kc2 * 2 + k2
                        for mt in range(2):
                            P.tr(bk.ap(c0=k2 * NMEM + mt * 128, c1=k2 * NMEM + mt * 128 + 128),
                                 kb_.ap(c0=mt * D + kc * 128, c1=mt * D + kc * 128 + 128), ident.ap(), [kb_, ident], [bk])
                    dst = kt.sub(kc2 * 512, 512)
                    P.copy(P.ev(), dst.ap(), bk.ap(), [bk], [dst])
                qsl = qp.v3(NTs)[:, :, b * LS:(b + 1) * LS]
                P.copy("pool", qsl, qT.v3(NTs)[:, :, b * LS:(b + 1) * LS], [qT, qp], [qp])
                bs = P.bank2()
                for h in range(4):
                    for k2 in range(2):
                        kc = 2 * h + k2
                        P.mm(bs.ap(0, NTs, h * NMEM, (h + 1) * NMEM), qp.ap(c0=kc * NTs, c1=(kc + 1) * NTs),
                             kt.ap(c0=kc * NMEM, c1=(kc + 1) * NMEM), k2 == 0, k2 == 1, [qp, kt], [bs])
                for half in range(2):
                    sd = sc.sub(half * 512, 512)
                    P.tt("dve", sd.ap(0, NTs), sd.ap(0, NTs), bs.ap(0, NTs, half * 512, half * 512 + 512), ALU.add,
                         [sd, bs], [sd])
                P.ts("pool", qsl, qT.v3(NTs)[:, :, b * LS:(b + 1) * LS], 0.0, None, ALU.mult, None, [qT, qp], [qp])
            softmax_rows(sc, NTs, pe, small)
            pT = ba(8 * NTs)
            vbf = [ba(2 * D), ba(2 * D)]
            bt = P.bank()
            for h in range(4):
                for mt in range(2):
                    o_ = (h * 2 + mt) * NTs
                    P.tr(bt.ap(c0=o_, c1=o_ + NTs), pe.ap(0, NTs, h * NMEM + mt * 128, h * NMEM + mt * 128 + 128),
                         ident.ap(0, NTs, 0, NTs), [pe, ident], [bt])
            P.copy("act", pT.ap(), bt.ap(), [bt], [pT])
            bo = P.bank()
            P.hold(bo)
            for b in range(NSQ):
                vb_ = kbuf[b % 4]
                P.dma("sp", vb_.v3(D), cv[b].rearrange("(t p) d -> p t d", p=128), writes=[vb_])
                vh = vbf[b % 2]
                P.copy("act", vh.ap(c1=D), vb_.ap(c1=D), [vb_], [vh.sub(0, D)])
                P.copy("dve", vh.ap(c0=D), vb_.ap(c0=D), [vb_], [vh.sub(D, D)])
                for oc in range(8):
                    h = oc // 2
                    for mt in range(2):
                        o_ = (h * 2 + mt) * NTs + b * LS
                        P.mm(bo.ap(c0=oc * NTs + b * LS, c1=oc * NTs + (b + 1) * LS),
                             vh.ap(c0=mt * D + oc * 128, c1=mt * D + oc * 128 + 128), pT.ap(c0=o_, c1=o_ + LS),
                             mt == 0, mt == 1, [vh, pT], [bo])
            P.copy("act", yT.ap(), bo.ap(), [bo], [yT])
            P.unhold(bo)
        AFr.release(m0)
        ARr.release(mR)
        ABr.release(mBa)

    def group(kind, g, last, pre=False, nxt=None):
        prompt = kind == "p"
        NT = NTP if prompt else NSQ * LS
        nseq, L = (1, NT) if prompt else (NSQ, LS)
        C = 128 if prompt else 64
        pt = min(128, NT)
        ntl = max(1, NT // 128)
        src_x = xp[g * NT:(g + 1) * NT, :] if prompt else xs
        dst_y = o_yp[g * NT:(g + 1) * NT, :] if prompt else o_ys
        mF = AFr.mark()
        mR = ARr.mark()
        jobs = Jobs()

        xs1 = F32View(Buf(ABr, 0, 2 * ntl * D))
        xs2 = F32View(Buf(ABr, 2 * ntl * D, 2 * ntl * D))
        hT = ra(KC * NT)
        U = ra(16 * NT)
        yT = U.sub(0, KC * NT)
        qT = U.sub(KC * NT, KC * NT)
        actT = U

        def phase0(_):
            m = AFr.mark()
            xT = fa(KC * NT)
            load_xT(src_x, NT, xT, stage=xs1 if pre else None)
            rmsnorm_fm(xT, vc("norm1_g"), NT, hT)
            AFr.release(m)
        jobs.add(None, phase0)

        mRW = AFr.mark()
        ss = fa(NSC * NT)
        SW = nseq * (L + 1)
        szraw = [fa(SW), fa(SW)]
        dtmp = [fa(NT), fa(NT)]
        stT = sso = None
        if not prompt:
            stT = fa(NSC * NSQ)
            sso = fa(NSC * NSQ)

            def load_state(_):
                m = AFr.mark()
                stok = fa(PSH)
                P.dma("sp", stok.ap(0, NSQ), sshift, writes=[stok])
                for sc in range(NSC):
                    bk = P.bank()
                    P.tr(bk.ap(c1=NSQ), stok.ap(0, NSQ, sc * 128, sc * 128 + 128), ident.ap(0, NSQ, 0, NSQ),
                         [stok, ident], [bk])
                    P.copy(P.ev(), stT.ap(c0=sc * NSQ, c1=(sc + 1) * NSQ), bk.ap(c1=NSQ), [bk],
                           [stT.sub(sc * NSQ, NSQ)])
                AFr.release(m)
            jobs.add(None, load_state)
        cnt = {"i": 0}

        def shift_sink(sc, bk):
            i = cnt["i"]
            cnt["i"] += 1
            sr = szraw[i % 2]
            dt_ = dtmp[i % 2]
            sr3 = sr.v3(L + 1)
            P.copy("act", sr3[:, :, 1:L + 1], bk.ap(c1=NT).rearrange("p (s l) -> p s l", l=L), [bk], [sr])
            if prompt:
                cc = carry.ap(c0=sc, c1=sc + 1).rearrange("p (s l) -> p s l", l=1)
                P.copy("act", sr3[:, :, 0:1], cc, [carry, sr], [sr])
                P.copy("act", cc, sr3[:, :, L:L + 1], [sr, carry], [carry])
            else:
                P.copy("pool", sr3[:, :, 0:1], stT.ap(c0=sc * NSQ, c1=(sc + 1) * NSQ).rearrange("p (s l) -> p s l", l=1),
                       [stT, sr], [sr])
                P.copy("pool", sso.ap(c0=sc * NSQ, c1=(sc + 1) * NSQ).rearrange("p (s l) -> p s l", l=1),
                       sr3[:, :, L:L + 1], [sr], [sso.sub(sc * NSQ, NSQ)])
            d3 = dt_.v3(L)
            P.tt("dve" if prompt else "pool", d3, sr3[:, :, 0:L], sr3[:, :, 1:L + 1], ALU.subtract, [sr], [dt_])
            dst = ss.sub(sc * NT, NT)
            P.stt("dve", dst.v3(L), d3, mucols.ap(c0=sc, c1=sc + 1), sr3[:, :, 1:L + 1], ALU.mult, ALU.add,
                  [dt_, mucols, sr], [dst])

        for blk in range(6):
            jobs.add(wsrc("w_in", 0, 2048 + blk * 512, 512),
                     lambda wb, blk=blk: linear_fm(wb, 512, hT, NT, shift_sink, oc0=blk * 4))
        jobs.add(wsrc("w_in", 0, 5120, 256), lambda wb: linear_fm(wb, 256, hT, NT, shift_sink, oc0=24))

        def rwkv_job(_):
            P.dump("ss%d" % g, ss.ap(), ss, [128, NSC * NT])
            rwkv_group(prompt, NT, nseq, L, C, ss, yT, g, sso, last)
        jobs.add(None, rwkv_job)

        merged = [None]

        def alloc_merged(_):
            AFr.release(mRW)
            merged[0] = fa(KC * NT)
        jobs.add(None, alloc_merged)

        def branch_jobs(n, first):
            gbuf = {}
            mk = {}

            def begin(_):
                mk["m"] = AFr.mark()
                for oc in range(8):
                    gbuf[oc] = fa(NT)

            def gate_job(half):
                def f(wb):
                    def sink(oc, bk):
                        P.act(gbuf[oc].ap(), bk.ap(c1=NT), AF.Sigmoid, [bk], [gbuf[oc]])
                    linear_fm(wb, 512, hT, NT, sink, oc0=half * 4)
                return f

            def proj_job(half):
                def f(wb):
                    def sink(oc, bk):
                        md = merged[0].sub(oc * NT, NT)
                        gt = gbuf[oc]
                        if first:
                            P.tt("dve", md.ap(), bk.ap(c1=NT), gt.ap(), ALU.mult, [bk, gt], [md])
                        else:
                            P.tt("dve", gt.ap(), bk.ap(c1=NT), gt.ap(), ALU.mult, [bk, gt], [gt])
                            P.tt("pool", md.ap(), md.ap(), gt.ap(), ALU.add, [md, gt], [md])
                    linear_fm(wb, 512, yT, NT, sink, oc0=half * 4)
                return f

            def end(_):
                AFr.release(mk["m"])
            jobs.add(None, begin)
            for half in range(2):
                jobs.add(wsrc("w_in", 0, 6400 + n * 1024 + half * 512, 512), gate_job(half))
            for half in range(2):
                jobs.add(wsrc("w_branch%d" % n, 0, half * 512, 512), proj_job(half))
            jobs.add(None, end)

        branch_jobs(1, True)

        gm = {}

        def gmlp_begin(_):
            gm["m"] = AFr.mark()
            gm["uT"] = fa(KC * NT)
            gm["vn"] = fa(ntl * D)
            gm["lng"] = fa(D)
            gm["lnb"] = fa(D)
            P.dma("sp", gm["lng"].ap(), ln_v_g.partition_broadcast(128), writes=[gm["lng"]])
            P.dma("sp", gm["lnb"].ap(), ln_v_b.partition_broadcast(128), writes=[gm["lnb"]])
            gm["s1"] = fa(2 * ntl)
            gm["WgT"] = fa(8 * C)
            gm["sgb"] = fa(8 * C)
            WgT = gm["WgT"]
            mI = cbuf("mI4") if prompt else cbuf("mI4s")
            if prompt:
                raw = fa(8 * 128)
                gm["raw"] = raw
                P.dma("sp", raw.v3(128), sg_w.rearrange("g t s -> t g s"), writes=[raw])
                P.dma("sp", gm["sgb"].ap(), sg_b.rearrange("g t -> (g t)").partition_broadcast(128),
                      writes=[gm["sgb"]])

                def build_wgt():
                    for gi in range(8):
                        bk = P.bank()
                        P.tr(bk.ap(c1=128), raw.ap(c0=gi * 128, c1=gi * 128 + 128), ident.ap(), [raw, ident], [bk])
                        P.tt("dve", WgT.ap(c0=gi * 128, c1=gi * 128 + 128), bk.ap(c1=128), mI.ap(c1=128), ALU.mult,
                             [bk, mI], [WgT.sub(gi * 128, 128)])
            else:
                w4t = fa(8 * 4)
                for gi in range(8):
                    P.dma("sp", w4t.ap(0, 4, gi * 4, gi * 4 + 4), sg_w[gi, 0:4, 0:4].rearrange("t s -> s t"),
                          writes=[w4t], allow_slow_non_contiguous=True)
                esel = cbuf("esel")
                bsb = fa(64)
                sgbp = fa(8 * 128)
                P.dma("sp", sgbp.ap(), sg_b.rearrange("g t -> (g t)").partition_broadcast(128), writes=[sgbp])

                def build_wgt():
                    for gi in range(8):
                        bk = P.bank()
                        P.mm(bk.ap(0, 4, 0, 64), w4t.ap(0, 4, gi * 4, gi * 4 + 4), esel.ap(0, 4, 0, 64), True, True,
                             [w4t, esel], [bk])
                        P.copy("act", bsb.ap(0, 4), bk.ap(0, 4, 0, 64), [bk], [bsb])
                        bk2 = P.bank()
                        P.mm(bk2.ap(0, 64, 0, 64), bsb.ap(0, 4), esel.ap(0, 4, 0, 64), True, True, [bsb, esel], [bk2])
                        P.tt("dve", WgT.ap(0, 64, gi * 64, gi * 64 + 64), bk2.ap(0, 64, 0, 64), mI.ap(0, 64, 0, 64),
                             ALU.mult, [bk2, mI], [WgT.sub(gi * 64, 64)])
                    P.copy("pool", gm["sgb"].v4(NSQ, LS), sgbp.pat([[128, 8], [0, NSQ], [1, LS]]), [sgbp], [gm["sgb"]])
            gm["build_wgt"] = build_wgt
        jobs.add(None, gmlp_begin)

        def u_job(half):
            def f(wb):
                def sink(oc, bk):
                    dst = gm["uT"].sub(oc * NT, NT)
                    P.act(dst.ap(), bk.ap(c1=NT), AF.Gelu_apprx_tanh, [bk], [dst])
                linear_fm(wb, 512, hT, NT, sink, oc0=half * 4)
            return f

        def v_job(half):
            def f(wb):
                for ti in range(ntl):
                    bk = P.bank()
                    for kc in range(KC):
                        P.mm(bk.ap(0, pt), hT.ap(c0=kc * NT + ti * 128, c1=kc * NT + ti * 128 + pt),
                             wb.ap(c0=kc * 512, c1=(kc + 1) * 512), kc == 0, kc == KC - 1, [hT, wb], [bk])
                    dst = gm["vn"].sub(ti * D + half * 512, 512)
                    s1c = gm["s1"].sub(ti * 2 + half, 1)
                    P.act(dst.ap(0, pt), bk.ap(0, pt), AF.Gelu_apprx_tanh, [bk], [dst, s1c], accum_out=s1c.ap(0, pt))
            return f
        for half in range(2):
            jobs.add(wsrc("w_in", 0, half * 512, 512), u_job(half))
        for half in range(2):
            jobs.add(wsrc("w_in", 0, 1024 + half * 512, 512), v_job(half))

        def gmlp_core(_):
            gm["build_wgt"]()
            m2 = AFr.mark()
            junk = fa(D)
            st2 = fa(8)
            tmpb = [fa(512), fa(512)]
            for ti in range(ntl):
                v = gm["vn"].sub(ti * D, D)
                s1 = gm["s1"].sub(ti * 2, 2)
                nm = st2.sub(0, 1)
                P.tt("dve", nm.ap(0, pt), s1.ap(0, pt, 0, 1), s1.ap(0, pt, 1, 2), ALU.add, [s1], [nm])
                P.ts("dve", nm.ap(0, pt), nm.ap(0, pt), -1.0 / D, None, ALU.mult, None, [nm], [nm])
                P.ts("dve", v.ap(0, pt), v.ap(0, pt), nm.ap(0, pt), None, ALU.add, None, [v, nm], [v])
                s2 = st2.sub(1, 1)
                P.act(junk.ap(0, pt), v.ap(0, pt), AF.Square, [v], [junk, s2], accum_out=s2.ap(0, pt))
                rs = st2.sub(2, 1)
                P.act(rs.ap(0, pt), s2.ap(0, pt), AF.Sqrt, [s2, epsb], [rs], bias=eps_ap(1, 0, pt), scale=1.0 / D)
                P.op("dve", lambda e, rs=rs: e.reciprocal(out=rs.ap(0, pt), in_=rs.ap(0, pt)), [rs], [rs])
                P.stt("dve", v.ap(0, pt), v.ap(0, pt), rs.ap(0, pt), gm["lng"].ap(0, pt), ALU.mult, ALU.mult,
                      [v, rs, gm["lng"]], [v])
                P.tt("pool", v.ap(0, pt), v.ap(0, pt), gm["lnb"].ap(0, pt), ALU.add, [v, gm["lnb"]], [v])
                if not prompt:
                    P.dma("sp", o_sgv, v.ap(0, pt), reads=[v])
                P.dump("vn%d_%d" % (g, ti), v.ap(0, pt), v, [pt, D])
                per = 512 // C
                nb_ = (8 + per - 1) // per
                banks = [P.bank() for _ in range(nb_)]
                for gi in range(8):
                    bk = banks[gi // per]
                    co = (gi % per) * C
                    P.mm(bk.ap(c0=co, c1=co + C), v.ap(0, pt, gi * 128, gi * 128 + 128),
                         gm["WgT"].ap(0, pt, gi * C, gi * C + C), True, True, [v, gm["WgT"]], [bk])
                for bi, bk in enumerate(banks):
                    ng = min(per, 8 - bi * per)
                    tmp = tmpb[bi % 2].sub(0, ng * C)
                    P.tt("dve", tmp.ap(), bk.ap(c1=ng * C), gm["sgb"].ap(c0=bi * per * C, c1=(bi * per + ng) * C),
                         ALU.add, [bk, gm["sgb"]], [tmp])
                    u3 = gm["uT"].v3(NT)[:, bi * per:bi * per + ng, ti * 128:ti * 128 + C]
                    y3 = yT.v3(NT)[:, bi * per:bi * per + ng, ti * 128:ti * 128 + C]
                    ur = Reg("F", [(gm["uT"].lo + (bi * per + k) * NT + ti * 128, C) for k in range(ng)])
                    yr = Reg("R", [(yT.lo + (bi * per + k) * NT + ti * 128, C) for k in range(ng)])
                    P.tt("pool", y3, tmp.v3(C), u3, ALU.mult, [tmp, ur], [yr])
            AFr.release(m2)
        jobs.add(None, gmlp_core)
        jobs.add(None, lambda _: AFr.release(gm["m"]))
        branch_jobs(0, False)

        def q_job(half):
            def f(wb):
                if half == 0 and prompt:
                    stage_x(src_x, NT, xs2)

                def sink(oc, bk):
                    dst = qT.sub(oc * NT, NT)
                    P.copy(P.ev(), dst.ap(), bk.ap(c1=NT), [bk], [dst])
                linear_fm(wb, 512, hT, NT, sink, oc0=half * 4)
            return f
        for half in range(2):
            jobs.add(wsrc("w_in", 0, 5376 + half * 512, 512), q_job(half))
        jobs.add(None, lambda _: attention(prompt, NT, qT, yT, g))
        branch_jobs(2, False)

        fin = {}

        def fin_begin(_):
            for kc in range(KC):
                P.copy(P.ev(), yT.ap(c0=kc * NT, c1=(kc + 1) * NT), merged[0].ap(c0=kc * NT, c1=(kc + 1) * NT),
                       [merged[0].sub(kc * NT, NT)], [yT.sub(kc * NT, NT)])
            P.dump("merged%d" % g, merged[0].ap(), merged[0], [128, KC * NT])
            AFr.release(mRW)
            fin["xT"] = fa(KC * NT)
            load_xT(src_x, NT, fin["xT"], stage=xs2 if prompt else None)
            if nxt is not None:
                stage_x(nxt, NT, xs1)
        jobs.add(None, fin_begin)

        def out_job(half):
            def f(wb):
                def sink(oc, bk):
                    x1 = fin["xT"].sub(oc * NT, NT)
                    P.tt("dve", x1.ap(), x1.ap(), bk.ap(c1=NT), ALU.add, [x1, bk], [x1])
                linear_fm(wb, 512, yT, NT, sink, oc0=half * 4)
            return f
        for half in range(2):
            jobs.add(wsrc("w_out", 0, half * 512, 512), out_job(half))

        def norm2(_):
            P.dump("x1T%d" % g, fin["xT"].ap(), fin["xT"], [128, KC * NT])
            rmsnorm_fm(fin["xT"], vc("norm2_g"), NT, hT)
            fin["r"] = [fa(NT), fa(NT)]
        jobs.add(None, norm2)

        def up_job(blk):
            def f(wb):
                def sink(oc, bk):
                    r = fin["r"][oc % 2]
                    P.act(r.ap(), bk.ap(c1=NT), AF.Relu, [bk], [r])
                    dst = actT.sub((oc % 16) * NT, NT)
                    P.stt("dve", dst.ap(), bk.ap(c1=NT), 0.0, r.ap(), ALU.max, ALU.mult, [bk, r], [dst])
                linear_fm(wb, 512, hT, NT, sink, oc0=blk * 4)
            return f

        dbanks = {}

        def down_job(cb, kb):
            def f(wb):
                kbl = kb % 2
                if kbl == 0:
                    dbanks[cb] = [P.bank() for _ in range(4)]
                for oc in range(4):
                    bk = dbanks[cb][oc]
                    for kc in range(8):
                        a = actT.sub((kbl * 8 + kc) * NT, NT)
                        P.mm(bk.ap(c1=NT), wb.ap(c0=kc * 512 + oc * 128, c1=kc * 512 + oc * 128 + 128), a.ap(),
                             kbl == 0 and kc == 0, kbl == 1 and kc == 7, [wb, a], [bk])
                    if kbl == 1:
                        x2 = fin["xT"].sub((cb * 4 + oc) * NT, NT)
                        P.tt("dve", x2.ap(), x2.ap(), bk.ap(c1=NT), ALU.add, [x2, bk], [x2])
            return f
        for hf in range(2):
            for blk in range(4):
                jobs.add(wsrc("w_up", 0, (hf * 4 + blk) * 512, 512), up_job(hf * 4 + blk))
            for cb in range(2):
                for kbl in range(2):
                    kb = hf * 2 + kbl
                    jobs.add(wsrc("w_down", kb * 1024, cb * 512, 512), down_job(cb, kb))

        def final(_):
            m2 = AFr.mark()
            yfin = fa(KC * NT)
            rmsnorm_fm(fin["xT"], vc("norm_f_g"), NT, yfin)
            ytok = fa(ntl * D)
            for ti in range(ntl):
                for half in range(2):
                    bk = P.bank()
                    for q in range(4):
                        kc = half * 4 + q
                        P.tr(bk.ap(0, pt, q * 128, q * 128 + 128), yfin.ap(c0=kc * NT + ti * 128, c1=kc * NT + ti * 128 + pt),
                             ident.ap(), [yfin, ident], [bk])
                    dst = ytok.sub(ti * D + half * 512, 512)
                    P.copy(P.ev(), dst.ap(0, pt), bk.ap(0, pt), [bk], [dst])
            if NT >= 128:
                P.dma("sp", dst_y.rearrange("(t p) d -> p t d", p=128), ytok.v3(D), reads=[ytok])
            else:
                P.dma("sp", dst_y, ytok.ap(0, pt), reads=[ytok])
            AFr.release(m2)
        jobs.add(None, final)

        jobs.run()
        AFr.release(mF)
        ARr.release(mR)

    for wname, ncolsT, krows in (("w_in", PIN, D), ("w_branch0", D, D), ("w_branch1", D, D), ("w_branch2", D, D),
                                ("w_out", D, D), ("w_up", DFF, D), ("w_down", D, DFF)):
        for r0 in range(0, krows, 1024):
            c0 = 0
            while c0 < ncolsT:
                nco = 512
                if wname == "w_in" and c0 == 5120:
                    nco = 256
                wsrc(wname, r0, c0, nco)
                c0 += nco
    assert len(wblocks) == NWB, len(wblocks)
    for g in range(ngroups):
        nx = xp[(g + 1) * NTP:(g + 2) * NTP, :] if g + 1 < ngroups else None
        group("p", g, g == ngroups - 1, pre=g > 0, nxt=nx)
    if do_sample:
        AFr.release(markF_sample)
        group("s", 0, True)

    P.S.finish()
    P.S.emit(nc)
    st.close()
    P.peakF = AFr.peak
    P.peakR = ARr.peak
    return P


_PROG = {}


def _core_inputs(c, inp):
    f = lambda a: np.ascontiguousarray(np.asarray(a, dtype=np.float32))
    m = {
        "xp": f(inp["x_prompt"][c]),
        "xs": f(inp["x_sample"][NSQ * c:NSQ * (c + 1)].reshape(NSQ * LS, D)),
        "memp": f(inp["mem_prompt"][c]),
        "ck": f(inp["cache_mem_k"][0, NSQ * c:NSQ * (c + 1)].reshape(NSQ, NMEM, D)),
        "cv": f(inp["cache_mem_v"][0, NSQ * c:NSQ * (c + 1)].reshape(NSQ, NMEM, D)),
        "sshift": f(inp["state_shift"][0, NSQ * c:NSQ * (c + 1)]),
        "swkv": f(inp["state_wkv"][0, NSQ * c:NSQ * (c + 1)]),
        "consts": CONSTS,
        "onesd": np.ones((128, 128), np.float32),
        "shift_mu": f(inp["shift_mu"][0]),
        "ln_v_g": f(inp["ln_v_g"][0]),
        "ln_v_b": f(inp["ln_v_b"][0]),
        "sg_w": f(inp["sg_w"][0]),
        "sg_b": f(inp["sg_b"][0]),
        "w_in": f(inp["w_in"][0]),
        "w_w2": f(inp["w_w2"][0]),
        "w_a2": f(inp["w_a2"][0]),
        "w_g2": f(inp["w_g2"][0]),
        "w_mem_k": f(inp["w_mem_k"][0]),
        "w_mem_v": f(inp["w_mem_v"][0]),
        "w_branch": f(inp["w_branch"][0]),
        "w_out": f(inp["w_out"][0]),
        "w_up": f(inp["w_up"][0]),
        "w_down": f(inp["w_down"][0]),
    }
    for n in VEC_NAMES:
        a = inp[n]
        a = a if n == "norm_f_g" else a[0]
        m[n] = f(a).reshape(D)
    return m


def kernel(**inputs):
    if "p" not in _PROG:
        _PROG["p"] = build_program()
    P = _PROG["p"]
    in_maps = [_core_inputs(c, inputs) for c in range(NCORES)]
    res = run_bass_kernel_spmd(P.nc, in_maps, core_ids=list(range(NCORES)))
    R = res.results
    g = lambda name: [np.asarray(R[c][name], dtype=np.float32) for c in range(NCORES)]
    y_prompt = np.stack(g("o_yp"), 0)
    y_sample = np.concatenate(g("o_ys"), 0).reshape(NCORES * NSQ, LS, D)
    prompt_shift = np.stack(g("o_pshift"), 0)[None]
    prompt_wkv = np.stack(g("o_pwkv"), 0)[None]
    prompt_mem_k = np.stack(g("o_pmk"), 0).reshape(1, NCORES, NMEM, 4, 256)
    prompt_mem_v = np.stack(g("o_pmv"), 0).reshape(1, NCORES, NMEM, 4, 256)
    sample_shift = np.concatenate(g("o_sshift"), 0)[None]
    sample_wkv = np.concatenate(g("o_swkv"), 0)[None]
    sample_gmlp_v = np.concatenate(g("o_sgv"), 0).reshape(1, NCORES * NSQ, LS, D)
    return (y_prompt, y_sample, prompt_shift, prompt_wkv, prompt_mem_k, prompt_mem_v,
            sample_shift, sample_wkv, sample_gmlp_v)
```
